# Optimizing a Trainium2 kernel written in Bass

```python
import jax, jax.numpy as jnp
from jax import lax
import numpy as np

D_MODEL = 2048
BATCH = 16
SEQ = 2048
DEPTH = 1

PLE_DIM = 256
GLA_HEADS = 4
GLA_DK = D_MODEL // 16
GLA_DV = D_MODEL // 8
GLA_RANK = 16
GLA_TAU = 16.0
GLA_CHUNK = 64
SWA_Q_HEADS = 16
SWA_KV_HEADS = 4
SWA_HEAD_DIM = D_MODEL // 32
WINDOW = 128
D_FF = 4 * D_MODEL
EPS = 1e-6
NEG = -1e30

BRANCH_A_WIDTH = GLA_HEADS * GLA_DV
BRANCH_B_WIDTH = SWA_Q_HEADS * SWA_HEAD_DIM
SPLIT_SIZES = [
    GLA_HEADS * GLA_DK,
    GLA_HEADS * GLA_DK,
    GLA_HEADS * GLA_DV,
    GLA_HEADS * GLA_DV,
    GLA_RANK,
    SWA_Q_HEADS * SWA_HEAD_DIM,
    SWA_KV_HEADS * SWA_HEAD_DIM,
    SWA_KV_HEADS * SWA_HEAD_DIM,
    D_MODEL,
    D_MODEL,
]
IN_WIDTH = sum(SPLIT_SIZES)
SPLITS = [int(s) for s in np.cumsum(SPLIT_SIZES)[:-1]]

kernel_name = "hybrid_gla_swa_sink_alibi_block"


def rmsnorm(x, g):
    xf = x.astype(jnp.float32)
    y = xf * lax.rsqrt(jnp.mean(xf * xf, axis=-1, keepdims=True) + EPS)
    return (y * g.astype(jnp.float32)).astype(x.dtype)


def gla_branch(q, k, v, log_a, r, gn_gain):
    B, S = q.shape[0], q.shape[1]
    nc = S // GLA_CHUNK
    C = GLA_CHUNK

    def to_chunks(t, d):
        return t.astype(jnp.float32).reshape(B, nc, C, GLA_HEADS, d).transpose(1, 0, 3, 2, 4)

    qc = to_chunks(q * (GLA_DK ** -0.5), GLA_DK)
    kc = to_chunks(k, GLA_DK)
    vc = to_chunks(v, GLA_DV)
    gc = to_chunks(log_a, GLA_DK)
    tri = jnp.tril(jnp.ones((C, C), dtype=bool))

    def step(state, inp):
        qi, ki, vi, gi = inp
        b = jnp.cumsum(gi, axis=2)
        o_inter = jnp.einsum('bhcd,bhde->bhce', qi * jnp.exp(b), state)
        diff = b[:, :, :, None, :] - b[:, :, None, :, :]
        decay = jnp.exp(jnp.where(tri[:, :, None], diff, -jnp.inf))
        attn = jnp.einsum('bhid,bhjd,bhijd->bhij', qi, ki, decay)
        o_intra = jnp.einsum('bhij,bhje->bhie', attn, vi)
        b_last = b[:, :, -1:, :]
        new_state = jnp.exp(b_last[:, :, 0, :])[..., None] * state + jnp.einsum(
            'bhcd,bhce->bhde', ki * jnp.exp(b_last - b), vi)
        return new_state, o_inter + o_intra

    state0 = jnp.zeros((B, GLA_HEADS, GLA_DK, GLA_DV), jnp.float32)
    _, o = lax.scan(step, state0, (qc, kc, vc, gc))
    o = o.transpose(1, 0, 3, 2, 4).reshape(B, S, GLA_HEADS, GLA_DV)
    o = o * lax.rsqrt(jnp.mean(o * o, axis=-1, keepdims=True) + EPS) * gn_gain.astype(jnp.float32)
    o = o.reshape(B, S, GLA_HEADS * GLA_DV)
    return (o * jax.nn.silu(r.astype(jnp.float32))).astype(r.dtype)


def swa_branch(q, k, v, sinks):
    B, S = q.shape[0], q.shape[1]
    nb = S // WINDOW
    G = SWA_Q_HEADS // SWA_KV_HEADS
    hd = SWA_HEAD_DIM
    qb = q.reshape(B, nb, WINDOW, SWA_KV_HEADS, G, hd)

    def banded(t):
        tb = t.reshape(B, nb, WINDOW, SWA_KV_HEADS, hd)
        prev = jnp.pad(tb[:, :-1], ((0, 0), (1, 0), (0, 0), (0, 0), (0, 0)))
        return jnp.concatenate([prev, tb], axis=2)

    kb, vb = banded(k), banded(v)
    scores = jnp.einsum('bnikgd,bnjkd->bnkgij', qb, kb).astype(jnp.float32) * (hd ** -0.5)
    qi = jnp.arange(WINDOW)[:, None] + WINDOW
    kj = jnp.arange(2 * WINDOW)[None, :]
    dist = qi - kj
    allowed = (dist >= 0) & (dist < WINDOW)
    blk = jnp.arange(nb)[:, None, None]
    valid = allowed[None] & ((blk > 0) | (kj[None] >= WINDOW))
    slopes = 2.0 ** (-8.0 * jnp.arange(1, SWA_Q_HEADS + 1, dtype=jnp.float32) / SWA_Q_HEADS)
    slopes = slopes.reshape(SWA_KV_HEADS, G)
    bias = -slopes[:, :, None, None] * dist.astype(jnp.float32)[None, None]
    scores = jnp.where(valid[None, :, None, None], scores + bias[None, None], NEG)
    sink = sinks.astype(jnp.float32).reshape(SWA_KV_HEADS, G)[:, :, None, None]
    m = jnp.maximum(jnp.max(scores, axis=-1, keepdims=True), sink)
    e = jnp.exp(scores - m)
    probs = e / (jnp.sum(e, axis=-1, keepdims=True) + jnp.exp(sink - m))
    out = jnp.einsum('bnkgij,bnjkd->bnikgd', probs.astype(v.dtype), vb)
    return out.reshape(B, S, SWA_Q_HEADS * hd)


def setup_inputs(seed: int = 0) -> dict:
    key = jax.random.key(seed)
    ks = jax.random.split(key, 20)
    f32 = jnp.float32

    def nrm(k, shape, scale):
        return jax.random.normal(k, shape, f32) * scale

    def gain(k, shape):
        return 1.0 + 0.05 * jax.random.normal(k, shape, f32)

    return {
        "x": nrm(ks[0], (BATCH, SEQ, D_MODEL), 1.0),
        "p": nrm(ks[1], (DEPTH, BATCH, SEQ, PLE_DIM), 1.0),
        "norm_mix": gain(ks[2], (DEPTH, D_MODEL)),
        "w_in": nrm(ks[3], (DEPTH, D_MODEL, IN_WIDTH), D_MODEL ** -0.5),
        "w_decay": nrm(ks[4], (DEPTH, GLA_RANK, GLA_HEADS * GLA_DK), GLA_RANK ** -0.5),
        "b_decay": nrm(ks[5], (DEPTH, GLA_HEADS * GLA_DK), 0.1),
        "gla_norm": gain(ks[6], (DEPTH, GLA_DV)),
        "attn_sinks": nrm(ks[7], (DEPTH, SWA_Q_HEADS), 0.5),
        "w_branch_a": nrm(ks[8], (DEPTH, BRANCH_A_WIDTH, D_MODEL), BRANCH_A_WIDTH ** -0.5),
        "w_branch_b": nrm(ks[9], (DEPTH, BRANCH_B_WIDTH, D_MODEL), BRANCH_B_WIDTH ** -0.5),
        "w_out": nrm(ks[10], (DEPTH, D_MODEL, D_MODEL), D_MODEL ** -0.5),
        "norm_mlp": gain(ks[11], (DEPTH, D_MODEL)),
        "w_up": nrm(ks[12], (DEPTH, D_MODEL, D_FF), D_MODEL ** -0.5),
        "w_down": nrm(ks[13], (DEPTH, D_FF, D_MODEL), D_FF ** -0.5),
        "norm_ple": gain(ks[14], (DEPTH, D_MODEL)),
        "w_ple_gate": nrm(ks[15], (DEPTH, D_MODEL, D_MODEL), D_MODEL ** -0.5),
        "w_ple_proj": nrm(ks[16], (DEPTH, PLE_DIM, D_MODEL), PLE_DIM ** -0.5),
        "norm_final": gain(ks[17], (D_MODEL,)),
    }


def reference(x, p, norm_mix, w_in, w_decay, b_decay, gla_norm, attn_sinks,
              w_branch_a, w_branch_b, w_out, norm_mlp, w_up, w_down,
              norm_ple, w_ple_gate, w_ple_proj, norm_final):
    B, S = x.shape[0], x.shape[1]
    h = x
    for i in range(DEPTH):
        u = rmsnorm(h, norm_mix[i])
        proj = u @ w_in[i]
        gq, gk, gv, gr, gz, sq, sk, sv, ga, gb = jnp.split(proj, SPLITS, axis=-1)
        log_a = jax.nn.log_sigmoid((gz @ w_decay[i] + b_decay[i]).astype(jnp.float32)) / GLA_TAU
        y_a = gla_branch(
            gq.reshape(B, S, GLA_HEADS, GLA_DK),
            gk.reshape(B, S, GLA_HEADS, GLA_DK),
            gv.reshape(B, S, GLA_HEADS, GLA_DV),
            log_a.reshape(B, S, GLA_HEADS, GLA_DK),
            gr, gla_norm[i])
        y_b = swa_branch(
            sq.reshape(B, S, SWA_Q_HEADS, SWA_HEAD_DIM),
            sk.reshape(B, S, SWA_KV_HEADS, SWA_HEAD_DIM),
            sv.reshape(B, S, SWA_KV_HEADS, SWA_HEAD_DIM),
            attn_sinks[i])
        merged = jax.nn.sigmoid(ga) * (y_a @ w_branch_a[i]) + jax.nn.sigmoid(gb) * (y_b @ w_branch_b[i])
        h = h + merged @ w_out[i]
        hn = rmsnorm(h, norm_mlp[i])
        h = h + jnp.square(jax.nn.relu(hn @ w_up[i])) @ w_down[i]
        hp = rmsnorm(h, norm_ple[i])
        h = h + jax.nn.sigmoid(hp @ w_ple_gate[i]) * (p[i] @ w_ple_proj[i])
    return rmsnorm(h, norm_final)
```

```python
import math
import numpy as np
import concourse.bass as bass
import concourse.mybir as mybir
from concourse.bass_utils import run_bass_kernel_spmd

F32 = mybir.dt.float32
BF16 = mybir.dt.bfloat16
AF = mybir.ActivationFunctionType
ALU = mybir.AluOpType
AX = mybir.AxisListType

D = 2048
T = 512
EPS = 1e-6
N_CORES = 8
SLOPES = [2.0 ** (-8.0 * (h + 1) / 16.0) for h in range(16)]

C_GAIN, C_SINK, C_GN, C_ID, C_TRI, C_DIST, C_END = 0, 64, 80, 336, 464, 976, 1232

A_QT, A_KT, A_V, A_G2, A_SQT, A_YAT, A_YBT = 0, 4096, 8192, 16384, 24576, 32768, 40960
A_SKX, A_SVX, A_GZX = 49152, 51712, 54272
A_GS = 55296
A_SS = 67584
A_END = 67584 + 2 * 16384 + 2 * 2048
RING = 3
SAME_ENG_WAR = False


def _esz(dt):
    return 2 if dt == BF16 else 4


def _iv(ap):
    pat = ap.ap
    row = pat[0][0]
    off = ap.offset
    lo = off % row if row > 0 else off
    ext = 1
    for s, c in pat[1:]:
        ext += (c - 1) * abs(s)
    e = _esz(ap.dtype)
    name = ap.tensor.name
    if name.startswith("bank"):
        return (name, 0, 2048)
    return (name, lo * e, (lo + ext) * e)


class _Eng:
    def __init__(self, name, sem):
        self.name = name
        self.sem = sem
        self.count = 0
        self.waited = {}
        self.ops = []


class Sched:
    def __init__(self, nc):
        self.nc = nc
        self.eng = {}
        for n in ("pe", "act", "dve", "pool", "sp"):
            self.eng[n] = _Eng(n, nc.alloc_semaphore("prog_" + n))
        self.segs = {}
        self.semobj = {}
        self.stage = ""

    def _split(self, key, x):
        L = self.segs.setdefault(key, [])
        for i, s in enumerate(L):
            if s[0] < x < s[1]:
                L.insert(i + 1, [x, s[1], s[2], dict(s[3])])
                s[1] = x
                return

    def _cover(self, key, lo, hi):
        self._split(key, lo)
        self._split(key, hi)
        return [s for s in self.segs.setdefault(key, []) if s[0] >= lo and s[1] <= hi]

    def _deps(self, reads, writes):
        toks = []
        for (key, lo, hi) in reads:
            for s in self._cover(key, lo, hi):
                if s[2] is not None:
                    toks.append((s[2], "w"))
                if key.startswith("bank"):
                    for sem, v in s[3].items():
                        toks.append(((sem, v), "r"))
        for (key, lo, hi) in writes:
            for s in self._cover(key, lo, hi):
                if s[2] is not None:
                    toks.append((s[2], "w"))
                for sem, v in s[3].items():
                    toks.append(((sem, v), "r"))
        return toks

    def _commit(self, tok, reads, writes):
        for (key, lo, hi) in reads:
            for s in self._cover(key, lo, hi):
                if s[3].get(tok[0], 0) < tok[1]:
                    s[3][tok[0]] = tok[1]
            self._fill(key, lo, hi, None, tok)
        for (key, lo, hi) in writes:
            L = self.segs.setdefault(key, [])
            self._split(key, lo)
            self._split(key, hi)
            L[:] = [s for s in L if not (s[0] >= lo and s[1] <= hi)]
            L.append([lo, hi, tok, {}])
            L.sort(key=lambda s: s[0])

    def _fill(self, key, lo, hi, w, rtok):
        L = self.segs.setdefault(key, [])
        cur = lo
        new = []
        for s in sorted(L, key=lambda s: s[0]):
            if s[1] <= lo or s[0] >= hi:
                continue
            if s[0] > cur:
                new.append([cur, s[0], w, {rtok[0]: rtok[1]}])
            cur = max(cur, s[1])
        if cur < hi:
            new.append([cur, hi, w, {rtok[0]: rtok[1]}])
        if new:
            L.extend(new)
            L.sort(key=lambda s: s[0])

    def _waits(self, e, toks):
        need = {}
        for (sem, v), kind in toks:
            if sem is e.sem:
                if e.name in ("pe", "sp") or (kind == "r" and not SAME_ENG_WAR):
                    continue
            if v > need.get(sem, 0):
                need[sem] = v
        out = []
        for sem, v in need.items():
            if e.waited.get(sem, 0) < v:
                e.waited[sem] = v
                out.append((sem, v))
        return out

    def op(self, en, fn, reads=(), writes=(), signal=True, extra=()):
        e = self.eng[en]
        r = [_iv(a) for a in reads]
        w = [_iv(a) for a in writes]
        toks = self._deps(r, w) + [(t, "w") for t in extra]
        waits = self._waits(e, toks)
        tok = (e.sem, e.count + 1)
        if signal:
            e.count += 1
        e.pending = not signal
        e.ops.append((waits, fn, (e.sem, 1) if signal else None, self.stage))
        self._commit(tok, r, w)
        return tok

    def dma(self, en, out, in_, dsem, sb_reads=(), sb_writes=(), extra=()):
        e = self.eng[en]
        r = [_iv(a) for a in sb_reads]
        w = [_iv(a) for a in sb_writes]
        toks = self._deps(r, w) + [(t, "w") for t in extra]
        if dsem[1] > 0:
            toks.append(((dsem[0], dsem[1]), "w"))
        waits = self._waits(e, toks)
        dsem[1] += 16
        tok = (dsem[0], dsem[1])
        e.ops.append((waits, (lambda g, o=out, i=in_: g.dma_start(out=o, in_=i)), (dsem[0], 16), self.stage))
        self._commit(tok, r, w)
        return tok

    def wait_only(self, en, toks):
        e = self.eng[en]
        waits = self._waits(e, [(t, "w") for t in toks])
        if waits:
            e.ops.append((waits, None, None, self.stage))

    def emit(self, en, g):
        for waits, fn, inc, _lab in self.eng[en].ops:
            for sem, v in waits:
                g.wait_ge(sem, v)
            if fn is None:
                continue
            ins = fn(g)
            if inc is not None:
                ins.then_inc(inc[0], inc[1])


def _slab_plan():
    P = []

    def simple(name, w, c0, kc=16, ncols=256, r0=0):
        P.append((name, kc, ncols, [(w, r0, c0, ncols, 0)]))

    for j in range(2):
        simple("gq%d" % j, "w_in", 256 * j)
    for j in range(2):
        simple("gk%d" % j, "w_in", 512 + 256 * j)
    for j in range(4):
        simple("gv%d" % j, "w_in", 1024 + 256 * j)
    for j in range(4):
        simple("gr%d" % j, "w_in", 2048 + 256 * j)
    for i in range(4):
        pcs = []
        for cc in range(2):
            c = 2 * i + cc
            ha = c if c < 4 else 8 + (c - 4)
            hb = ha + 4
            pcs.append(("w_in", 0, 3088 + 64 * ha, 64, 128 * cc))
            pcs.append(("w_in", 0, 3088 + 64 * hb, 64, 128 * cc + 64))
        P.append(("sq%d" % i, 16, 256, pcs))
    simple("sk", "w_in", 4112)
    simple("sv", "w_in", 4368)
    for j in range(8):
        simple("A%d" % j, "w_branch_a", 256 * j, kc=8, ncols=256)
        simple("ga%d" % j, "w_in", 4624 + 256 * j)
        simple("B%d" % j, "w_branch_b", 256 * j, kc=8, ncols=256)
        simple("gb%d" % j, "w_in", 6672 + 256 * j)
    for j in range(8):
        simple("wo%d" % j, "w_out", 256 * j)
    for q in range(4):
        for j in range(8):
            simple("up%d_%d" % (q, j), "w_up", 2048 * q + 256 * j)
        for j in range(8):
            simple("dn%d_%d" % (q, j), "w_down", 256 * j, r0=16 * q)
    for j in range(8):
        simple("pp%d" % j, "w_ple_proj", 256 * j, kc=2, ncols=256)
        simple("pg%d" % j, "w_ple_gate", 256 * j)
    return P


W_SHAPES = {
    "w_in": (2048, 8720), "w_branch_a": (1024, 2048), "w_branch_b": (1024, 2048),
    "w_out": (2048, 2048), "w_up": (2048, 8192), "w_down": (8192, 2048),
    "w_ple_gate": (2048, 2048), "w_ple_proj": (256, 2048),
}


nc_sched_holder = []


def build(n_seq=2, seq_len=2048, debug=None):
    NTOK = n_seq * seq_len
    NT = NTOK // T
    TPS = seq_len // T
    nc = bass.Bass("TRN2", target_bir_lowering=False)
    x_d = nc.dram_tensor("x", [NTOK, D], F32, kind="ExternalInput").ap()
    p_d = nc.dram_tensor("p", [NTOK, 256], F32, kind="ExternalInput").ap()
    cst_d = nc.dram_tensor("cst", [128, C_END], F32, kind="ExternalInput").ap()
    wdx_d = nc.dram_tensor("wdx", [17, 512], F32, kind="ExternalInput").ap()
    w_d = {n: nc.dram_tensor(n, list(s), F32, kind="ExternalInput").ap() for n, s in W_SHAPES.items()}
    out_d = nc.dram_tensor("out", [NTOK, D], F32, kind="ExternalOutput").ap()
    plan = _slab_plan()
    NS = len(plan)
    wsc = nc.dram_tensor("wsc", [NS, 128, 4096], BF16).ap()
    wgz_d = nc.dram_tensor("wgzsc", [128, 256], BF16).ap()

    S = Sched(nc)
    A = nc.alloc_sbuf_tensor
    hT = A("hT", [128, 16, T], F32)
    uT = A("uT", [128, 16, T], BF16)
    ring = A("ring", [128, RING, 4096], BF16)
    cst = A("cst_sb", [128, C_END], F32)
    identb = A("identb", [128, 128], BF16)
    onesb = A("onesb", [128, 128], BF16)
    wdec = A("wdec", [32, 512], BF16)
    wgz = A("wgz", [128, 16, 16], BF16)
    Sst = A("Sst", [128, 4, 256], F32)
    Sbf = A("Sbf", [128, 4, 256], BF16)
    pT = A("pT", [128, 2, T], BF16)
    sqs = A("sqs", [128, 4, T], BF16)
    fA = A("fA", [128, 3, T], F32)
    st = A("stats", [128, 144], F32)
    lnv = fA[:, 2, :]
    arena = A("arena", [128, A_END // 2], BF16)
    banks = [nc.alloc_psum_tensor("bank%d" % i, [128, 512], F32) for i in range(8)]

    def av(off, nbytes, dt, pat=None, parts=128, **kw):
        a = arena[0:parts, off // 2:(off + nbytes) // 2]
        if dt != BF16:
            a = a.bitcast(dt)
        if pat:
            a = a.rearrange(pat, **kw)
        return a

    qT = av(A_QT, 4096, BF16, "p (a b) -> p a b", a=4)
    kT = av(A_KT, 4096, BF16, "p (a b) -> p a b", a=4)
    vt = av(A_V, 8192, BF16, "p (a b) -> p a b", a=4)
    mgT = av(A_QT, 16384, BF16, "p (a b) -> p a b", a=16)
    G2 = av(A_G2, 8192, BF16, "p (a b) -> p a b", a=4)
    sqT = av(A_SQT, 8192, BF16, "p (a b) -> p a b", a=8)
    yaT = av(A_YAT, 8192, BF16, "p (a b) -> p a b", a=8)
    ybT = av(A_YBT, 8192, BF16, "p (a b) -> p a b", a=8)
    skx = av(A_SKX, 2560, BF16, "p (a b) -> p a b", a=2)
    svx = av(A_SVX, 2560, BF16, "p (a b) -> p a b", a=5)
    gzx = av(A_GZX, 1024, BF16, parts=32)
    gL = av(A_GS, 2048, F32)
    gE1 = av(A_GS + 2048, 2048, F32)
    gE2 = av(A_GS + 4096, 2048, F32)
    qtl = av(A_GS + 6144, 1024, BF16, "p (a b) -> p a b", a=4)
    ktl = av(A_GS + 7168, 1024, BF16, "p (a b) -> p a b", a=4)
    ktm = av(A_GS + 8192, 1024, BF16)
    gAT = av(A_GS + 9216, 1024, BF16, "p (a b) -> p a b", a=4)
    yat = av(A_GS + 10240, 2048, BF16)
    xs = av(A_GS, 32768, F32, "p (a b) -> p a b", a=4)
    pst = av(A_GS + 32768, 4096, F32, "p (a b) -> p a b", a=4)
    osb = av(A_GS + 36864, 8192, F32)
    zz = [av(A_SS + 16384 * i, 8192, F32, "p (a b) -> p a b", a=8) for i in range(2)]
    Pb = [av(A_SS + 16384 * i + 8192, 4096, BF16, "p (a b) -> p a b", a=8) for i in range(2)]
    PTb = [av(A_SS + 16384 * i + 12288, 4096, BF16, "p (a b) -> p a b", a=16) for i in range(2)]
    ybn = [av(A_SS + 32768 + 2048 * i, 2048, BF16) for i in range(2)]
    actb = [av(16384 * i, 16384, BF16, "p (a b) -> p a b", a=16) for i in range(2)]
    oT = av(0, 32768, F32, "p (a b) -> p a b", a=16)

    gains = cst[:, C_GAIN:C_GAIN + 64]
    sinks = cst[:, C_SINK:C_SINK + 16]
    gnb = cst[:, C_GN:C_GN + 256]
    identf = cst[:, C_ID:C_ID + 128]
    tri4 = cst[:, C_TRI:C_TRI + 512]
    trif = cst[:, C_TRI:C_TRI + 128]
    dist8 = cst[:, C_DIST:C_DIST + 256]

    bank_i = [0]

    def bank():
        b = banks[bank_i[0] % 7]
        bank_i[0] += 1
        return b

    ssq_bank = banks[7]

    sq_pending = []

    def sq_accum(k, delay=3):
        act(sqs[:, k % 4, :], hT[:, k, :], AF.Square)
        sq_pending.append(k)
        while len(sq_pending) > delay:
            sq_flush(1)

    def sq_flush(n=100):
        while sq_pending and n > 0:
            k = sq_pending.pop(0)
            n -= 1
            mm(ssq_bank[:, :], lhsT=onesb[:], rhs=sqs[:, k % 4, :], start=(k == 0), stop=(k == 15), signal=True)

    def mm(out, lhsT, rhs, start=True, stop=True, signal=True):
        return S.op("pe", lambda g: g.matmul(out, lhsT=lhsT, rhs=rhs, start=start, stop=stop),
                    reads=[lhsT, rhs], writes=[out], signal=signal)

    def tr(out, in_, ident, signal=True):
        return S.op("pe", lambda g: g.transpose(out, in_, ident), reads=[in_, ident], writes=[out], signal=signal)

    def act(out, in_, func, bias=None, scale=None, accum=None, eng="act"):
        kw = {}
        rd = [in_]
        wr = [out]
        if bias is not None:
            kw["bias"] = bias
            if not isinstance(bias, (int, float)):
                rd.append(bias)
        if scale is not None:
            kw["scale"] = scale
            if not isinstance(scale, (int, float)):
                rd.append(scale)
        if accum is not None:
            kw["accum_out"] = accum
            wr.append(accum)
        return S.op("act", lambda g: g.activation(out=out, in_=in_, func=func, **kw), reads=rd, writes=wr)

    def copy(eng, out, in_):
        if eng == "act":
            return act(out, in_, AF.Copy)
        return S.op(eng, lambda g: g.tensor_copy(out=out, in_=in_), reads=[in_], writes=[out])

    def tt(out, in0, in1, op, eng="dve"):
        return S.op(eng, lambda g: g.tensor_tensor(out=out, in0=in0, in1=in1, op=op), reads=[in0, in1], writes=[out])

    def ts(out, in0, s1, s2, op0, op1=None, eng="dve"):
        rd = [in0] + [s for s in (s1, s2) if s is not None and not isinstance(s, (int, float))]
        if op1 is None:
            return S.op(eng, lambda g: g.tensor_scalar(out=out, in0=in0, scalar1=s1, scalar2=None, op0=op0),
                        reads=rd, writes=[out])
        return S.op(eng, lambda g: g.tensor_scalar(out=out, in0=in0, scalar1=s1, scalar2=s2, op0=op0, op1=op1),
                    reads=rd, writes=[out])

    def stt(out, in0, scalar, in1, op0, op1):
        rd = [in0, in1] + ([] if isinstance(scalar, (int, float)) else [scalar])
        return S.op("dve", lambda g: g.scalar_tensor_tensor(out=out, in0=in0, scalar=scalar, in1=in1, op0=op0, op1=op1),
                    reads=rd, writes=[out])

    def memset(ap, v, eng="dve"):
        return S.op(eng, lambda g: g.memset(ap, v), writes=[ap])

    csem = [nc.alloc_semaphore("cld"), 0]
    S.dma("sp", cst[:], cst_d[:, :], csem, sb_writes=[cst[:]])
    csem2 = [nc.alloc_semaphore("cld2"), 0]
    S.dma("sp", lnv[0:17, :], wdx_d[:, :], csem2, sb_writes=[lnv[0:17, :]])
    copy("dve", identb[:], identf)
    memset(onesb[:], 1.0)
    copy("dve", wdec[0:17, :], lnv[0:17, :])
    nsink = st[:, 128:144]
    ts(nsink, sinks, -1.0, None, ALU.mult)
    memset(gzx, 1.0)

    cv = [[nc.alloc_semaphore("cv%d" % i), 0] for i in range(8)]
    cvn = [0]

    def conv(dst, src):
        i = cvn[0]
        cvn[0] += 1
        return S.dma("pool", dst, src, cv[i % 8])

    gzt = conv(wgz_d.rearrange("p (kc c) -> p kc c", kc=16),
               w_d["w_in"].rearrange("(kc p) c -> p kc c", p=128)[:, :, 3072:3088])
    gsem = [nc.alloc_semaphore("gzl"), 0]
    S.dma("sp", wgz[:], wgz_d.rearrange("p (kc c) -> p kc c", kc=16), gsem, sb_writes=[wgz[:]], extra=[gzt])

    slab_ready = []
    for s, (name, kc, ncols, pcs) in enumerate(plan):
        toks = []
        dv = wsc[s][:, 0:kc * ncols].rearrange("p (kc c) -> p kc c", kc=kc)
        for (wn, r0, c0, n, d0) in pcs:
            src = w_d[wn].rearrange("(kc p) c -> p kc c", p=128)[:, r0:r0 + kc, c0:c0 + n]
            toks.append(conv(dv[:, :, d0:d0 + n], src))
        slab_ready.append(toks)

    rsem = [[nc.alloc_semaphore("ring%d" % i), 0] for i in range(RING)]
    ld = {"n": 0}
    total_slabs = NT * NS
    slab_views = {}

    def issue_slab():
        i = ld["n"]
        if i >= total_slabs:
            return
        ld["n"] += 1
        s = i % NS
        slot = i % RING
        extra = slab_ready[s] if i < NS else ()
        n_el = plan[s][1] * plan[s][2]
        S.dma("sp", ring[:, slot, 0:n_el], wsc[s][:, 0:n_el], rsem[slot], sb_writes=[ring[:, slot, 0:n_el]], extra=extra)

    use = {"n": 0}

    def next_slab(expect, issue=True):
        i = use["n"]
        use["n"] += 1
        s = i % NS
        name, kc, ncols, _ = plan[s]
        assert name == expect, (name, expect)
        if issue:
            issue_slab()
        return ring[:, i % RING, 0:kc * ncols].rearrange("p (kc c) -> p kc c", kc=kc)

    for _ in range(RING - 1):
        issue_slab()

    xsem = [[nc.alloc_semaphore("xs%d" % i), 0] for i in range(8)]
    psem = [nc.alloc_semaphore("pld"), 0]
    osem = [[nc.alloc_semaphore("ost%d" % i), 0] for i in range(2)]
    dbg_toks = []

    def dump(name, ap, dt):
        if not debug:
            return
        shp = list(ap.shape)
        d = nc.dram_tensor("dbg_" + name, shp, dt, kind="ExternalOutput").ap()
        sem = [nc.alloc_semaphore("dbg_" + name), 0]
        dbg_toks.append(S.dma("sp", d, ap, sem, sb_reads=[ap]))

    def load_x(t, m, hf):
        r0 = t * T + m * 128
        dst = xs[:, m, hf * 1024:(hf + 1) * 1024]
        S.dma("sp", dst, x_d[r0:r0 + 128, hf * 1024:(hf + 1) * 1024], xsem[m * 2 + hf], sb_writes=[dst])

    def norm(gcol, writer):
        sq_flush()
        act(lnv, ssq_bank[:, :], AF.Ln, bias=EPS, scale=1.0 / D)
        rb = bank()
        act(rb[:, :], lnv, AF.Exp, scale=-0.5)
        for k in range(16):
            writer(k, rb[:, :], gains[:, gcol + k:gcol + k + 1])

    def norm_to_uT(gcol):
        norm(gcol, lambda k, r, g: stt(uT[:, k, :], hT[:, k, :], g, r, ALU.mult, ALU.mult))

    ev = {"n": 0}

    def evac_eng():
        ev["n"] += 1
        return "act" if ev["n"] % 2 == 0 else "dve"

    def dense_b_pair(sl0, sl1, kc, rhs_of_k):
        pbs = [bank() for _ in range(4)]
        for k in range(kc):
            for c4 in range(4):
                sl = sl0 if c4 < 2 else sl1
                mm(pbs[c4][:, :], lhsT=sl[:, k, 128 * (c4 % 2):128 * (c4 % 2) + 128], rhs=rhs_of_k(k),
                   start=(k == 0), stop=(k == kc - 1), signal=(k == kc - 1))
        return pbs

    def dense_b(slab, kc, c0, rhs_of_k):
        pb = bank()
        for k in range(kc):
            mm(pb[:, :], lhsT=slab[:, k, c0:c0 + 128], rhs=rhs_of_k(k), start=(k == 0), stop=(k == kc - 1),
               signal=(k == kc - 1))
        return pb

    def s1_gen(t):
        first = (t % TPS == 0)
        S.stage = "S1"
        if t == 0:
            for m_ in range(4):
                for hf_ in range(2):
                    load_x(0, m_, hf_)
        S.dma("sp", pst, p_d[t * T:(t + 1) * T, :].rearrange("(m p) c -> p m c", p=128), psem, sb_writes=[pst])
        for m in range(4):
            for kg in range(4):
                S.stage = "S1"
                pb = bank()
                for j in range(4):
                    k = 4 * kg + j
                    tr(pb[:, j * 128:(j + 1) * 128], xs[:, m, k * 128:(k + 1) * 128], identf, signal=(j == 3))
                copy(evac_eng(), hT[:, 4 * kg:4 * kg + 4, m * 128:(m + 1) * 128],
                     pb[:, :].rearrange("p (a b) -> p a b", a=4))
                if m == 3:
                    for k in range(4 * kg, 4 * kg + 4):
                        sq_accum(k)
                yield
            S.stage = "S1"
            pb = bank()
            for j in range(2):
                tr(pb[:, j * 128:(j + 1) * 128], pst[:, m, j * 128:(j + 1) * 128], identf, signal=(j == 1))
            copy(evac_eng(), pT[:, :, m * 128:(m + 1) * 128], pb[:, 0:256].rearrange("p (a b) -> p a b", a=2))
        if first:
            memset(Sst[:], 0.0)
            memset(Sbf[:], 0.0)
        else:
            copy("dve", skx[:, :, 0:128], skx[:, :, 512:640])
            copy("dve", svx[:, 0, :], svx[:, 4, :])

    def s10_gen(t):
        for m in range(4):
            for kg in range(4):
                S.stage = "S10"
                pb = bank()
                for j in range(4):
                    k = 4 * kg + j
                    tr(pb[:, j * 128:(j + 1) * 128], oT[:, k, m * 128:(m + 1) * 128], identf, signal=(j == 3))
                copy(evac_eng(), osb[:, kg * 512:(kg + 1) * 512], pb[:, :])
                if kg % 2 == 1:
                    hf = kg // 2
                    r0 = t * T + m * 128
                    S.dma("sp", out_d[r0:r0 + 128, hf * 1024:(hf + 1) * 1024], osb[:, hf * 1024:(hf + 1) * 1024],
                          osem[hf], sb_reads=[osb[:, hf * 1024:(hf + 1) * 1024]])
                yield

    for _ in s1_gen(0):
        pass
    for t in range(NT):
        first = (t % TPS == 0)
        norm_to_uT(0)

        S.stage = "S2"
        sl0 = next_slab("gq0")
        sl1 = next_slab("gq1", issue=False)
        pbs = dense_b_pair(sl0, sl1, 16, lambda k: uT[:, k, :])
        issue_slab()
        for c4 in range(4):
            copy(evac_eng(), qT[:, c4, :], pbs[c4][:, :])
        for j in range(2):
            sl = next_slab("gk%d" % j)
            for c in range(2):
                pb = dense_b(sl, 16, 128 * c, lambda k: uT[:, k, :])
                copy(evac_eng(), kT[:, 2 * j + c, :], pb[:, :])

        def dense_a(sl, m, half, pb):
            for k in range(16):
                mm(pb[:, half * 256:(half + 1) * 256], lhsT=uT[:, k, m * 128:(m + 1) * 128], rhs=sl[:, k, :],
                   start=(k == 0), stop=(k == 15), signal=(k == 15))

        for j in range(4):
            sl = next_slab("gv%d" % j)
            for mp in range(2):
                pb = bank()
                for hf in range(2):
                    dense_a(sl, 2 * mp + hf, hf, pb)
                copy(evac_eng(), vt[:, 2 * mp:2 * mp + 2, 256 * j:256 * (j + 1)],
                     pb[:, :].rearrange("p (a b) -> p a b", a=2))
        for j in range(4):
            sl = next_slab("gr%d" % j)
            for mp in range(2):
                pb = bank()
                for hf in range(2):
                    dense_a(sl, 2 * mp + hf, hf, pb)
                act(fA[:, 0, :], pb[:, :], AF.Tanh, scale=0.5)
                stt(fA[:, 1, :], fA[:, 0, :], 1.0, pb[:, :], ALU.add, ALU.mult)
                gsl = gnb[:, (256 * j) % 256:(256 * j) % 256 + 256]
                for hf in range(2):
                    tt(G2[:, 2 * mp + hf, 256 * j:256 * (j + 1)], fA[:, 1, hf * 256:(hf + 1) * 256], gsl, ALU.mult)
        pb = bank()
        for k in range(16):
            mm(pb[0:16, :], lhsT=wgz[:, k, :], rhs=uT[:, k, :], start=(k == 0), stop=(k == 15), signal=(k == 15))
        copy("dve", gzx[0:16, :], pb[0:16, :])
        S.stage = "S3"
        def dense_swa_gen():
            for i in range(4):
                sl = next_slab("sq%d" % i)
                for c in range(2):
                    pb = dense_b(sl, 16, 128 * c, lambda k: uT[:, k, :])
                    copy(evac_eng(), sqT[:, 2 * i + c, :], pb[:, :])
                    yield
            sl = next_slab("sk")
            for c in range(2):
                pb = dense_b(sl, 16, 128 * c, lambda k: uT[:, k, :])
                copy(evac_eng(), skx[:, c, 128:640], pb[:, :])
                yield
            sl = next_slab("sv")
            for mp in range(2):
                pb = bank()
                for hf in range(2):
                    dense_a(sl, 2 * mp + hf, hf, pb)
                copy(evac_eng(), svx[:, 1 + 2 * mp:3 + 2 * mp, :], pb[:, :].rearrange("p (a b) -> p a b", a=2))
                yield

        def gla_gen():
            for m in range(4):
                tk = slice(m * 128, (m + 1) * 128)
                pz = bank()
                mm(pz[:, :], lhsT=gzx[0:17, tk], rhs=wdec[0:17, :])
                act(gE2, pz[:, :], AF.Exp, scale=-1.0)
                act(gL, gE2, AF.Ln, bias=1.0)
                yield
                pbt = bank()
                for h in range(4):
                    mm(pbt[:, h * 128:(h + 1) * 128], lhsT=gL[:, h * 128:(h + 1) * 128], rhs=trif, signal=(h == 3))
                act(gE1, pbt[:, :], AF.Exp, scale=-1.0 / 16.0)
                act(gE2, pbt[:, :], AF.Exp, scale=1.0 / 16.0)
                yield
                stt(qtl, qT[:, :, tk], 128.0 ** -0.5, gE1.rearrange("p (a b) -> p a b", a=4), ALU.mult, ALU.mult)
                tt(ktl, kT[:, :, tk], gE2.rearrange("p (a b) -> p a b", a=4), ALU.mult)
                yield
                pk = bank()
                pkb = pk[:, 0:256].bitcast(BF16)
                for h in range(4):
                    tr(pkb[:, h * 128:(h + 1) * 128], ktl[:, h, :], identb[:], signal=(h == 3))
                copy("act", ktm, pkb)
                pa = bank()
                for h in range(4):
                    mm(pa[:, h * 128:(h + 1) * 128], lhsT=ktl[:, h, :], rhs=qtl[:, h, :], signal=(h == 3))
                tt(gAT, pa[:, :].rearrange("p (a b) -> p a b", a=4), tri4.rearrange("p (a b) -> p a b", a=4), ALU.mult)
                yield
                po = [bank(), bank()]
                for h in range(4):
                    o = po[h // 2][:, (h % 2) * 256:(h % 2 + 1) * 256]
                    mm(o, lhsT=gAT[:, h, :], rhs=vt[:, m, h * 256:(h + 1) * 256], start=True, stop=False, signal=False)
                    mm(o, lhsT=qtl[:, h, :], rhs=Sbf[:, h, :], start=False, stop=True, signal=(h % 2 == 1))
                pd = [bank(), bank()]
                for h in range(4):
                    mm(pd[h // 2][:, (h % 2) * 256:(h % 2 + 1) * 256], lhsT=ktm[:, h * 128:(h + 1) * 128],
                       rhs=vt[:, m, h * 256:(h + 1) * 256], signal=(h % 2 == 1))
                for h in range(4):
                    o = po[h // 2][:, (h % 2) * 256:(h % 2 + 1) * 256]
                    act(fA[:, 2, 0:256], o, AF.Square, accum=st[:, h:h + 1])
                act(st[:, 4:8], st[:, 0:4], AF.Ln, bias=EPS, scale=1.0 / 256.0)
                act(st[:, 8:12], st[:, 4:8], AF.Exp, scale=-0.5, bias=math.log(0.5))
                for h in range(4):
                    o = po[h // 2][:, (h % 2) * 256:(h % 2 + 1) * 256]
                    stt(yat[:, h * 256:(h + 1) * 256], o, st[:, 8 + h:9 + h], G2[:, m, h * 256:(h + 1) * 256],
                        ALU.mult, ALU.mult)
                yield
                for i2 in range(2):
                    tt(Sst[:, 2 * i2:2 * i2 + 2, :], Sst[:, 2 * i2:2 * i2 + 2, :],
                       pd[i2][:, :].rearrange("p (a b) -> p a b", a=2), ALU.add)
                dl = gE1[:, 127:128]
                dec = bass.AP(dl.tensor, dl.offset, [list(dl.ap[0]), [128, 4], [0, 256]])
                tt(Sst[:], Sst[:], dec, ALU.mult)
                copy("act", Sbf[:], Sst[:])
                yield
                pt_ = bank()
                ptb = pt_[:, :].bitcast(BF16)
                for c in range(8):
                    tr(ptb[:, c * 128:(c + 1) * 128], yat[:, c * 128:(c + 1) * 128], identb[:], signal=(c == 7))
                copy("act", yaT[:, :, tk], ptb.rearrange("p (a b) -> p a b", a=8))
                yield

        gg = gla_gen()
        gla_live = [True]

        def gla_steps(n):
            for _ in range(n):
                if gla_live[0]:
                    try:
                        next(gg)
                    except StopIteration:
                        gla_live[0] = False

        for _ in dense_swa_gen():
            gla_steps(2)

        def swa_P1(u):
            m, kvp = u // 2, u % 2
            tk = slice(m * 128, (m + 1) * 128)
            k0 = 128 if (first and m == 0) else 0
            b = u % 2
            so = 16 + 56 * kvp
            ps_ = [bank() for _ in range(4)]
            for jj in (0, 4, 1, 5, 2, 6, 3, 7):
                hf, j = jj // 4, jj % 4
                prow = slice(hf * 64, (hf + 1) * 64)
                c = kvp * 4 + j
                mm(ps_[jj // 2][:, (jj % 2) * 256 + k0:(jj % 2 + 1) * 256], lhsT=sqT[prow, c, tk],
                   rhs=skx[prow, kvp, m * 128 + k0:m * 128 + 256], signal=True)
            for jj in range(8):
                h = 8 * kvp + jj
                stt(zz[b][:, jj, k0:256], dist8[:, k0:256], -SLOPES[h],
                    ps_[jj // 2][:, (jj % 2) * 256 + k0:(jj % 2 + 1) * 256], ALU.mult, ALU.add)
            S.op("dve", lambda g, o=st[:, so:so + 8], i=zz[b][:, :, k0:256]: g.tensor_reduce(out=o, in_=i, axis=AX.X, op=ALU.max),
                 reads=[zz[b][:, :, k0:256]], writes=[st[:, so:so + 8]])

        def swa_P1b(u):
            m, kvp = u // 2, u % 2
            k0 = 128 if (first and m == 0) else 0
            b = u % 2
            so = 16 + 56 * kvp
            stt(st[:, so + 8:so + 16], st[:, so:so + 8], -0.125, nsink[:, 8 * kvp:8 * kvp + 8], ALU.mult, ALU.min)
            tt(st[:, so + 16:so + 24], sinks[:, 8 * kvp:8 * kvp + 8], st[:, so + 8:so + 16], ALU.add)
            act(st[:, so + 24:so + 32], st[:, so + 16:so + 24], AF.Exp)
            for jj in range(8):
                act(Pb[b][:, jj, k0:256], zz[b][:, jj, k0:256], AF.Exp, bias=st[:, so + 8 + jj:so + 9 + jj], scale=0.125,
                    accum=st[:, so + 32 + jj:so + 33 + jj])

        def swa_P2(u):
            m, kvp = u // 2, u % 2
            k0 = 128 if (first and m == 0) else 0
            b = u % 2
            so = 16 + 56 * kvp
            tt(st[:, so + 40:so + 48], st[:, so + 24:so + 32], st[:, so + 32:so + 40], ALU.add)
            S.op("dve", lambda g, o=st[:, so + 48:so + 56], i=st[:, so + 40:so + 48]: g.reciprocal(out=o, in_=i),
                 reads=[st[:, so + 40:so + 48]], writes=[st[:, so + 48:so + 56]])
            halves = [1] if k0 else [0, 1]
            n_tr = 4 * len(halves)
            for i2 in range(2):
                pp_ = bank()
                ppb = pp_[:, :].bitcast(BF16)
                cnt = 0
                for j in range(4):
                    jj = 4 * i2 + j
                    for hh in halves:
                        cnt += 1
                        tr(ppb[:, (2 * j + hh) * 128:(2 * j + hh + 1) * 128], Pb[b][:, jj, hh * 128:(hh + 1) * 128],
                           identb[:], signal=(cnt == n_tr))
                dst = PTb[b][:, 8 * i2:8 * i2 + 8, :]
                eng = "act"
                if k0:
                    copy(eng, dst.rearrange("p (j h) q -> p j h q", h=2)[:, :, 1, :],
                         ppb.rearrange("p (j h q) -> p j h q", j=4, h=2)[:, :, 1, :])
                else:
                    copy(eng, dst, ppb.rearrange("p (a b) -> p a b", a=8))

        def swa_P3(u):
            m, kvp = u // 2, u % 2
            tk = slice(m * 128, (m + 1) * 128)
            k0 = 128 if (first and m == 0) else 0
            b = u % 2
            so = 16 + 56 * kvp
            halves = [1] if k0 else [0, 1]
            py = bank()
            for jj in range(8):
                kv = 2 * kvp + jj // 4
                o = py[:, jj * 64:(jj + 1) * 64]
                for ii, hh in enumerate(halves):
                    mm(o, lhsT=PTb[b][:, 2 * jj + hh, :], rhs=svx[:, m + hh, kv * 64:(kv + 1) * 64],
                       start=(ii == 0), stop=(ii == len(halves) - 1),
                       signal=(ii == len(halves) - 1 and jj == 7))
            yb_ = ybn[m % 2]
            tt(yb_[:, 8 * kvp * 64:(8 * kvp + 8) * 64].rearrange("p (a b) -> p a b", a=8),
               py[:, :].rearrange("p (a b) -> p a b", a=8),
               st[:, so + 48:so + 56].unsqueeze(2).broadcast_to([128, 8, 64]), ALU.mult)
            if kvp == 1:
                pt_ = bank()
                ptb = pt_[:, :].bitcast(BF16)
                for c in range(8):
                    tr(ptb[:, c * 128:(c + 1) * 128], yb_[:, c * 128:(c + 1) * 128], identb[:], signal=(c == 7))
                copy("dve", ybT[:, :, tk], ptb.rearrange("p (a b) -> p a b", a=8))

        for s_ in range(8 + 3):
            if s_ < 8:
                swa_P1(s_)
            if 0 <= s_ - 2 < 8:
                swa_P2(s_ - 2)
            if 0 <= s_ - 3 < 8:
                swa_P3(s_ - 3)
            if s_ < 8:
                swa_P1b(s_)
            gla_steps(2)
        gla_steps(100)

        if t == 0:
            dump("uT", uT[:], BF16)
            dump("yaT", yaT, BF16)
            dump("ybT", ybT, BF16)
            dump("qT", qT, BF16)
            dump("vt", vt, BF16)
            dump("G2", G2, BF16)
            dump("sqT", sqT, BF16)
            dump("skx", skx, BF16)
            dump("svx", svx, BF16)
        S.stage = "S4"
        for j in range(8):
            slA = next_slab("A%d" % j)
            pA = [dense_b(slA, 8, 128 * c, lambda k: yaT[:, k, :]) for c in range(2)]
            sga = next_slab("ga%d" % j)
            for c in range(2):
                pg = dense_b(sga, 16, 128 * c, lambda k: uT[:, k, :])
                act(fA[:, 0, :], pg[:, :], AF.Tanh, scale=0.5)
                stt(fA[:, 1 + c, :], fA[:, 0, :], 1.0, pA[c][:, :], ALU.add, ALU.mult)
            slB = next_slab("B%d" % j)
            pB = [dense_b(slB, 8, 128 * c, lambda k: ybT[:, k, :]) for c in range(2)]
            sgb = next_slab("gb%d" % j)
            for c in range(2):
                f = 2 * j + c
                pg = dense_b(sgb, 16, 128 * c, lambda k: uT[:, k, :])
                act(fA[:, 0, :], pg[:, :], AF.Tanh, scale=0.5)
                stt(fA[:, 0, :], fA[:, 0, :], 1.0, pB[c][:, :], ALU.add, ALU.mult)
                tt(mgT[:, f, :], fA[:, 0, :], fA[:, 1 + c, :], ALU.add)

        if t == 0:
            dump("mgT", mgT, BF16)
        S.stage = "S5"
        for j in range(8):
            sl = next_slab("wo%d" % j)
            for c in range(2):
                f = 2 * j + c
                pb = dense_b(sl, 16, 128 * c, lambda k: mgT[:, k, :])
                stt(hT[:, f, :], pb[:, :], 0.5, hT[:, f, :], ALU.mult, ALU.add)
                sq_accum(f)

        if t == 0:
            dump("h1", hT[:], F32)
        S.stage = "S6-8"
        norm_to_uT(16)
        if t + 1 < NT:
            for m_ in range(4):
                for hf_ in range(2):
                    load_x(t + 1, m_, hf_)
        for q in range(4):
            ab = actb[q % 2]
            j_start = 0
            if q == 0:
                sl0 = next_slab("up0_0")
                sl1 = next_slab("up0_1", issue=False)
                pbs = dense_b_pair(sl0, sl1, 16, lambda k: uT[:, k, :])
                issue_slab()
                for c4 in range(4):
                    act(fA[:, c4 % 3, :], pbs[c4][:, :], AF.Square)
                    stt(ab[:, c4, :], pbs[c4][:, :], 0.0, fA[:, c4 % 3, :], ALU.is_gt, ALU.mult)
                j_start = 2
            for j in range(j_start, 8):
                sl = next_slab("up%d_%d" % (q, j))
                for c in range(2):
                    pb = dense_b(sl, 16, 128 * c, lambda k: uT[:, k, :])
                    act(fA[:, (2 * j + c) % 3, :], pb[:, :], AF.Square)
                    stt(ab[:, 2 * j + c, :], pb[:, :], 0.0, fA[:, (2 * j + c) % 3, :], ALU.is_gt, ALU.mult)
            for j in range(8):
                sl = next_slab("dn%d_%d" % (q, j))
                for c in range(2):
                    f = 2 * j + c
                    pb = dense_b(sl, 16, 128 * c, lambda k: ab[:, k, :])
                    tt(hT[:, f, :], pb[:, :], hT[:, f, :], ALU.add)
                    if q == 3:
                        sq_accum(f)

        if t == 0:
            dump("h2", hT[:], F32)
        S.stage = "S9"
        norm_to_uT(32)
        for j in range(8):
            slp = next_slab("pp%d" % j)
            ppb_ = [dense_b(slp, 2, 128 * c, lambda k: pT[:, k, :]) for c in range(2)]
            sl = next_slab("pg%d" % j)
            for c in range(2):
                f = 2 * j + c
                pp_ = ppb_[c]
                pg = dense_b(sl, 16, 128 * c, lambda k: uT[:, k, :])
                act(fA[:, c, :], pg[:, :], AF.Tanh, scale=0.5)
                stt(fA[:, c, :], fA[:, c, :], 1.0, pp_[:, :], ALU.add, ALU.mult)
                stt(hT[:, f, :], fA[:, c, :], 0.5, hT[:, f, :], ALU.mult, ALU.add)
                sq_accum(f)

        if t == 0:
            dump("h3", hT[:], F32)
        S.stage = "S10"
        norm(48, lambda k, r, g: stt(oT[:, k, :], hT[:, k, :], g, r, ALU.mult, ALU.mult))
        g10 = s10_gen(t)
        g1 = s1_gen(t + 1) if t + 1 < NT else iter(())
        live = [True, True]
        while live[0] or live[1]:
            for gi, gg_ in enumerate((g10, g1)):
                if live[gi]:
                    try:
                        next(gg_)
                    except StopIteration:
                        live[gi] = False

    S.wait_only("sp", [(o_[0], o_[1]) for o_ in osem] + dbg_toks)
    for en in ("pe", "act", "dve"):
        assert not getattr(S.eng[en], "pending", False), en

    nc_sched_holder.clear()
    nc_sched_holder.append(S)
    with nc.Block() as block:
        @block.sync
        def _(g):
            S.emit("sp", g)

        @block.gpsimd
        def _(g):
            S.emit("pool", g)

        @block.tensor
        def _(g):
            S.emit("pe", g)

        @block.scalar
        def _(g):
            S.emit("act", g)

        @block.vector
        def _(g):
            S.emit("dve", g)
    return nc


def make_consts(norm_mix, norm_mlp, norm_ple, norm_final, attn_sinks, gla_norm):
    c = np.zeros((128, C_END), np.float32)
    for n, g in enumerate((norm_mix, norm_mlp, norm_ple, norm_final)):
        c[:, C_GAIN + 16 * n:C_GAIN + 16 * (n + 1)] = np.asarray(g, np.float32).reshape(16, 128).T
    c[:, C_SINK:C_SINK + 16] = np.asarray(attn_sinks, np.float32).reshape(1, 16)
    c[:, C_GN:C_GN + 256] = np.asarray(gla_norm, np.float32).reshape(1, 256)
    c[:, C_ID:C_ID + 128] = np.eye(128, dtype=np.float32)
    tri = (np.arange(128)[:, None] <= np.arange(128)[None, :]).astype(np.float32)
    c[:, C_TRI:C_TRI + 512] = np.tile(tri, (1, 4))
    i = np.arange(128)[:, None]
    j = np.arange(256)[None, :]
    dist = i + 128 - j
    ok = (dist >= 0) & (dist < 128)
    c[:, C_DIST:C_DIST + 256] = np.where(ok, 8.0 * dist, 1e9).astype(np.float32)
    return c


_NC_CACHE = {}


def kernel(x, p, norm_mix, w_in, w_decay, b_decay, gla_norm, attn_sinks, w_branch_a, w_branch_b,
           w_out, norm_mlp, w_up, w_down, norm_ple, w_ple_gate, w_ple_proj, norm_final):
    x = np.asarray(x, np.float32)
    B, SEQ, _ = x.shape
    n_cores = N_CORES
    per = B // n_cores
    key = (per, SEQ)
    if key not in _NC_CACHE:
        _NC_CACHE[key] = build(n_seq=per, seq_len=SEQ)
    nc = _NC_CACHE[key]
    cst = make_consts(np.asarray(norm_mix)[0], np.asarray(norm_mlp)[0], np.asarray(norm_ple)[0],
                      np.asarray(norm_final), np.asarray(attn_sinks)[0], np.asarray(gla_norm)[0])
    wdx = np.concatenate([np.asarray(w_decay, np.float32)[0], np.asarray(b_decay, np.float32)[0][None, :]], axis=0)
    wts = {
        "w_in": np.ascontiguousarray(np.asarray(w_in, np.float32)[0]),
        "w_branch_a": np.ascontiguousarray(np.asarray(w_branch_a, np.float32)[0]),
        "w_branch_b": np.ascontiguousarray(np.asarray(w_branch_b, np.float32)[0]),
        "w_out": np.ascontiguousarray(np.asarray(w_out, np.float32)[0]),
        "w_up": np.ascontiguousarray(np.asarray(w_up, np.float32)[0]),
        "w_down": np.ascontiguousarray(np.asarray(w_down, np.float32)[0]),
        "w_ple_gate": np.ascontiguousarray(np.asarray(w_ple_gate, np.float32)[0]),
        "w_ple_proj": np.ascontiguousarray(np.asarray(w_ple_proj, np.float32)[0]),
    }
    pp = np.asarray(p, np.float32)[0]
    in_maps = []
    for c in range(n_cores):
        m = {"x": np.ascontiguousarray(x[c * per:(c + 1) * per].reshape(per * SEQ, D)),
             "p": np.ascontiguousarray(pp[c * per:(c + 1) * per].reshape(per * SEQ, 256)),
             "cst": cst, "wdx": wdx}
        m.update(wts)
        in_maps.append(m)
    res = run_bass_kernel_spmd(nc, in_maps, core_ids=list(range(n_cores)))
    out = np.concatenate([np.asarray(r["out"]).reshape(per, SEQ, D) for r in res.results], axis=0)
    return out.astype(np.float32)
```

```python
import math
import numpy as np
import concourse.bass as bass
import concourse.mybir as mybir
from concourse.bass_utils import run_bass_kernel_spmd

F32 = mybir.dt.float32
BF16 = mybir.dt.bfloat16
AF = mybir.ActivationFunctionType
ALU = mybir.AluOpType
AX = mybir.AxisListType

D = 2048
T = 512
EPS = 1e-6
N_CORES = 8
SLOPES = [2.0 ** (-8.0 * (h + 1) / 16.0) for h in range(16)]

C_GAIN, C_SINK, C_GN, C_ID, C_TRI, C_DIST, C_END = 0, 64, 80, 336, 464, 976, 1232

A_QT, A_KT, A_V, A_G2, A_SQT, A_YAT, A_YBT = 0, 4096, 8192, 16384, 24576, 32768, 40960
A_SKX, A_SVX, A_GZX = 49152, 51712, 54272
A_GS = 55296
A_SS = 67584
A_END = 67584 + 2 * 16384 + 2 * 2048
RING = 3
SAME_ENG_WAR = False


def _esz(dt):
    return 2 if dt == BF16 else 4


def _iv(ap):
    pat = ap.ap
    row = pat[0][0]
    off = ap.offset
    lo = off % row if row > 0 else off
    ext = 1
    for s, c in pat[1:]:
        ext += (c - 1) * abs(s)
    e = _esz(ap.dtype)
    name = ap.tensor.name
    if name.startswith("bank"):
        return (name, 0, 2048)
    return (name, lo * e, (lo + ext) * e)


class _Eng:
    def __init__(self, name, sem):
        self.name = name
        self.sem = sem
        self.count = 0
        self.waited = {}
        self.ops = []


class Sched:
    def __init__(self, nc):
        self.nc = nc
        self.eng = {}
        for n in ("pe", "act", "dve", "pool", "sp"):
            self.eng[n] = _Eng(n, nc.alloc_semaphore("prog_" + n))
        self.segs = {}
        self.semobj = {}
        self.stage = ""

    def _split(self, key, x):
        L = self.segs.setdefault(key, [])
        for i, s in enumerate(L):
            if s[0] < x < s[1]:
                L.insert(i + 1, [x, s[1], s[2], dict(s[3])])
                s[1] = x
                return

    def _cover(self, key, lo, hi):
        self._split(key, lo)
        self._split(key, hi)
        return [s for s in self.segs.setdefault(key, []) if s[0] >= lo and s[1] <= hi]

    def _deps(self, reads, writes):
        toks = []
        for (key, lo, hi) in reads:
            for s in self._cover(key, lo, hi):
                if s[2] is not None:
                    toks.append((s[2], "w"))
                if key.startswith("bank"):
                    for sem, v in s[3].items():
                        toks.append(((sem, v), "r"))
        for (key, lo, hi) in writes:
            for s in self._cover(key, lo, hi):
                if s[2] is not None:
                    toks.append((s[2], "w"))
                for sem, v in s[3].items():
                    toks.append(((sem, v), "r"))
        return toks

    def _commit(self, tok, reads, writes):
        for (key, lo, hi) in reads:
            for s in self._cover(key, lo, hi):
                if s[3].get(tok[0], 0) < tok[1]:
                    s[3][tok[0]] = tok[1]
            self._fill(key, lo, hi, None, tok)
        for (key, lo, hi) in writes:
            L = self.segs.setdefault(key, [])
            self._split(key, lo)
            self._split(key, hi)
            L[:] = [s for s in L if not (s[0] >= lo and s[1] <= hi)]
            L.append([lo, hi, tok, {}])
            L.sort(key=lambda s: s[0])

    def _fill(self, key, lo, hi, w, rtok):
        L = self.segs.setdefault(key, [])
        cur = lo
        new = []
        for s in sorted(L, key=lambda s: s[0]):
            if s[1] <= lo or s[0] >= hi:
                continue
            if s[0] > cur:
                new.append([cur, s[0], w, {rtok[0]: rtok[1]}])
            cur = max(cur, s[1])
        if cur < hi:
            new.append([cur, hi, w, {rtok[0]: rtok[1]}])
        if new:
            L.extend(new)
            L.sort(key=lambda s: s[0])

    def _waits(self, e, toks):
        need = {}
        for (sem, v), kind in toks:
            if sem is e.sem:
                if e.name in ("pe", "sp") or (kind == "r" and not SAME_ENG_WAR):
                    continue
            if v > need.get(sem, 0):
                need[sem] = v
        out = []
        for sem, v in need.items():
            if e.waited.get(sem, 0) < v:
                e.waited[sem] = v
                out.append((sem, v))
        return out

    def op(self, en, fn, reads=(), writes=(), signal=True, extra=()):
        e = self.eng[en]
        r = [_iv(a) for a in reads]
        w = [_iv(a) for a in writes]
        toks = self._deps(r, w) + [(t, "w") for t in extra]
        waits = self._waits(e, toks)
        tok = (e.sem, e.count + 1)
        if signal:
            e.count += 1
        e.pending = not signal
        e.ops.append((waits, fn, (e.sem, 1) if signal else None, self.stage))
        self._commit(tok, r, w)
        return tok

    def dma(self, en, out, in_, dsem, sb_reads=(), sb_writes=(), extra=()):
        e = self.eng[en]
        r = [_iv(a) for a in sb_reads]
        w = [_iv(a) for a in sb_writes]
        toks = self._deps(r, w) + [(t, "w") for t in extra]
        if dsem[1] > 0:
            toks.append(((dsem[0], dsem[1]), "w"))
        waits = self._waits(e, toks)
        dsem[1] += 16
        tok = (dsem[0], dsem[1])
        e.ops.append((waits, (lambda g, o=out, i=in_: g.dma_start(out=o, in_=i)), (dsem[0], 16), self.stage))
        self._commit(tok, r, w)
        return tok

    def wait_only(self, en, toks):
        e = self.eng[en]
        waits = self._waits(e, [(t, "w") for t in toks])
        if waits:
            e.ops.append((waits, None, None, self.stage))

    def emit(self, en, g):
        for waits, fn, inc, _lab in self.eng[en].ops:
            for sem, v in waits:
                g.wait_ge(sem, v)
            if fn is None:
                continue
            ins = fn(g)
            if inc is not None:
                ins.then_inc(inc[0], inc[1])


def _slab_plan():
    P = []

    def simple(name, w, c0, kc=16, ncols=256, r0=0):
        P.append((name, kc, ncols, [(w, r0, c0, ncols, 0)]))

    for j in range(2):
        simple("gq%d" % j, "w_in", 256 * j)
    for j in range(2):
        simple("gk%d" % j, "w_in", 512 + 256 * j)
    for j in range(4):
        simple("gv%d" % j, "w_in", 1024 + 256 * j)
    for j in range(4):
        simple("gr%d" % j, "w_in", 2048 + 256 * j)
    for i in range(4):
        pcs = []
        for cc in range(2):
            c = 2 * i + cc
            ha = c if c < 4 else 8 + (c - 4)
            hb = ha + 4
            pcs.append(("w_in", 0, 3088 + 64 * ha, 64, 128 * cc))
            pcs.append(("w_in", 0, 3088 + 64 * hb, 64, 128 * cc + 64))
        P.append(("sq%d" % i, 16, 256, pcs))
    simple("sk", "w_in", 4112)
    simple("sv", "w_in", 4368)
    for j in range(8):
        simple("A%d" % j, "w_branch_a", 256 * j, kc=8, ncols=256)
        simple("ga%d" % j, "w_in", 4624 + 256 * j)
        simple("B%d" % j, "w_branch_b", 256 * j, kc=8, ncols=256)
        simple("gb%d" % j, "w_in", 6672 + 256 * j)
    for j in range(8):
        simple("wo%d" % j, "w_out", 256 * j)
    for q in range(4):
        for j in range(8):
            simple("up%d_%d" % (q, j), "w_up", 2048 * q + 256 * j)
        for j in range(8):
            simple("dn%d_%d" % (q, j), "w_down", 256 * j, r0=16 * q)
    for j in range(8):
        simple("pp%d" % j, "w_ple_proj", 256 * j, kc=2, ncols=256)
        simple("pg%d" % j, "w_ple_gate", 256 * j)
    return P


W_SHAPES = {
    "w_in": (2048, 8720), "w_branch_a": (1024, 2048), "w_branch_b": (1024, 2048),
    "w_out": (2048, 2048), "w_up": (2048, 8192), "w_down": (8192, 2048),
    "w_ple_gate": (2048, 2048), "w_ple_proj": (256, 2048),
}


nc_sched_holder = []


def build(n_seq=2, seq_len=2048, debug=None):
    NTOK = n_seq * seq_len
    NT = NTOK // T
    TPS = seq_len // T
    nc = bass.Bass("TRN2", target_bir_lowering=False)
    x_d = nc.dram_tensor("x", [NTOK, D], F32, kind="ExternalInput").ap()
    p_d = nc.dram_tensor("p", [NTOK, 256], F32, kind="ExternalInput").ap()
    cst_d = nc.dram_tensor("cst", [128, C_END], F32, kind="ExternalInput").ap()
    wdx_d = nc.dram_tensor("wdx", [17, 512], F32, kind="ExternalInput").ap()
    w_d = {n: nc.dram_tensor(n, list(s), F32, kind="ExternalInput").ap() for n, s in W_SHAPES.items()}
    out_d = nc.dram_tensor("out", [NTOK, D], F32, kind="ExternalOutput").ap()
    plan = _slab_plan()
    NS = len(plan)
    wsc = nc.dram_tensor("wsc", [NS, 128, 4096], BF16).ap()
    wgz_d = nc.dram_tensor("wgzsc", [128, 256], BF16).ap()

    S = Sched(nc)
    A = nc.alloc_sbuf_tensor
    hT = A("hT", [128, 16, T], F32)
    uT = A("uT", [128, 16, T], BF16)
    ring = A("ring", [128, RING, 4096], BF16)
    cst = A("cst_sb", [128, C_END], F32)
    identb = A("identb", [128, 128], BF16)
    onesb = A("onesb", [128, 128], BF16)
    wdec = A("wdec", [32, 512], BF16)
    wgz = A("wgz", [128, 16, 16], BF16)
    Sst = A("Sst", [128, 4, 256], F32)
    Sbf = A("Sbf", [128, 4, 256], BF16)
    pT = A("pT", [128, 2, T], BF16)
    sqs = A("sqs", [128, 4, T], BF16)
    fA = A("fA", [128, 3, T], F32)
    st = A("stats", [128, 144], F32)
    lnv = fA[:, 2, :]
    arena = A("arena", [128, A_END // 2], BF16)
    banks = [nc.alloc_psum_tensor("bank%d" % i, [128, 512], F32) for i in range(8)]

    def av(off, nbytes, dt, pat=None, parts=128, **kw):
        a = arena[0:parts, off // 2:(off + nbytes) // 2]
        if dt != BF16:
            a = a.bitcast(dt)
        if pat:
            a = a.rearrange(pat, **kw)
        return a

    qT = av(A_QT, 4096, BF16, "p (a b) -> p a b", a=4)
    kT = av(A_KT, 4096, BF16, "p (a b) -> p a b", a=4)
    vt = av(A_V, 8192, BF16, "p (a b) -> p a b", a=4)
    mgT = av(A_QT, 16384, BF16, "p (a b) -> p a b", a=16)
    G2 = av(A_G2, 8192, BF16, "p (a b) -> p a b", a=4)
    sqT = av(A_SQT, 8192, BF16, "p (a b) -> p a b", a=8)
    yaT = av(A_YAT, 8192, BF16, "p (a b) -> p a b", a=8)
    ybT = av(A_YBT, 8192, BF16, "p (a b) -> p a b", a=8)
    skx = av(A_SKX, 2560, BF16, "p (a b) -> p a b", a=2)
    svx = av(A_SVX, 2560, BF16, "p (a b) -> p a b", a=5)
    gzx = av(A_GZX, 1024, BF16, parts=32)
    gL = av(A_GS, 2048, F32)
    gE1 = av(A_GS + 2048, 2048, F32)
    gE2 = av(A_GS + 4096, 2048, F32)
    qtl = av(A_GS + 6144, 1024, BF16, "p (a b) -> p a b", a=4)
    ktl = av(A_GS + 7168, 1024, BF16, "p (a b) -> p a b", a=4)
    ktm = av(A_GS + 8192, 1024, BF16)
    gAT = av(A_GS + 9216, 1024, BF16, "p (a b) -> p a b", a=4)
    yat = av(A_GS + 10240, 2048, BF16)
    xs = av(A_GS, 32768, F32, "p (a b) -> p a b", a=4)
    pst = av(A_GS + 32768, 4096, F32, "p (a b) -> p a b", a=4)
    osb = av(A_GS + 36864, 8192, F32)
    zz = [av(A_SS + 16384 * i, 8192, F32, "p (a b) -> p a b", a=8) for i in range(2)]
    Pb = [av(A_SS + 16384 * i + 8192, 4096, BF16, "p (a b) -> p a b", a=8) for i in range(2)]
    PTb = [av(A_SS + 16384 * i + 12288, 4096, BF16, "p (a b) -> p a b", a=16) for i in range(2)]
    ybn = [av(A_SS + 32768 + 2048 * i, 2048, BF16) for i in range(2)]
    actb = [av(16384 * i, 16384, BF16, "p (a b) -> p a b", a=16) for i in range(2)]
    oT = av(0, 32768, F32, "p (a b) -> p a b", a=16)

    gains = cst[:, C_GAIN:C_GAIN + 64]
    sinks = cst[:, C_SINK:C_SINK + 16]
    gnb = cst[:, C_GN:C_GN + 256]
    identf = cst[:, C_ID:C_ID + 128]
    tri4 = cst[:, C_TRI:C_TRI + 512]
    trif = cst[:, C_TRI:C_TRI + 128]
    dist8 = cst[:, C_DIST:C_DIST + 256]

    bank_i = [0]

    def bank():
        b = banks[bank_i[0] % 7]
        bank_i[0] += 1
        return b

    ssq_bank = banks[7]

    sq_pending = []

    def sq_accum(k, delay=3):
        act(sqs[:, k % 4, :], hT[:, k, :], AF.Square)
        sq_pending.append(k)
        while len(sq_pending) > delay:
            sq_flush(1)

    def sq_flush(n=100):
        while sq_pending and n > 0:
            k = sq_pending.pop(0)
            n -= 1
            mm(ssq_bank[:, :], lhsT=onesb[:], rhs=sqs[:, k % 4, :], start=(k == 0), stop=(k == 15), signal=True)

    def mm(out, lhsT, rhs, start=True, stop=True, signal=True):
        return S.op("pe", lambda g: g.matmul(out, lhsT=lhsT, rhs=rhs, start=start, stop=stop),
                    reads=[lhsT, rhs], writes=[out], signal=signal)

    def tr(out, in_, ident, signal=True):
        return S.op("pe", lambda g: g.transpose(out, in_, ident), reads=[in_, ident], writes=[out], signal=signal)

    def act(out, in_, func, bias=None, scale=None, accum=None, eng="act"):
        kw = {}
        rd = [in_]
        wr = [out]
        if bias is not None:
            kw["bias"] = bias
            if not isinstance(bias, (int, float)):
                rd.append(bias)
        if scale is not None:
            kw["scale"] = scale
            if not isinstance(scale, (int, float)):
                rd.append(scale)
        if accum is not None:
            kw["accum_out"] = accum
            wr.append(accum)
        return S.op("act", lambda g: g.activation(out=out, in_=in_, func=func, **kw), reads=rd, writes=wr)

    def copy(eng, out, in_):
        if eng == "act":
            return act(out, in_, AF.Copy)
        return S.op(eng, lambda g: g.tensor_copy(out=out, in_=in_), reads=[in_], writes=[out])

    def tt(out, in0, in1, op, eng="dve"):
        return S.op(eng, lambda g: g.tensor_tensor(out=out, in0=in0, in1=in1, op=op), reads=[in0, in1], writes=[out])

    def ts(out, in0, s1, s2, op0, op1=None, eng="dve"):
        rd = [in0] + [s for s in (s1, s2) if s is not None and not isinstance(s, (int, float))]
        if op1 is None:
            return S.op(eng, lambda g: g.tensor_scalar(out=out, in0=in0, scalar1=s1, scalar2=None, op0=op0),
                        reads=rd, writes=[out])
        return S.op(eng, lambda g: g.tensor_scalar(out=out, in0=in0, scalar1=s1, scalar2=s2, op0=op0, op1=op1),
                    reads=rd, writes=[out])

    def stt(out, in0, scalar, in1, op0, op1):
        rd = [in0, in1] + ([] if isinstance(scalar, (int, float)) else [scalar])
        return S.op("dve", lambda g: g.scalar_tensor_tensor(out=out, in0=in0, scalar=scalar, in1=in1, op0=op0, op1=op1),
                    reads=rd, writes=[out])

    def memset(ap, v, eng="dve"):
        return S.op(eng, lambda g: g.memset(ap, v), writes=[ap])

    csem = [nc.alloc_semaphore("cld"), 0]
    S.dma("sp", cst[:], cst_d[:, :], csem, sb_writes=[cst[:]])
    csem2 = [nc.alloc_semaphore("cld2"), 0]
    S.dma("sp", lnv[0:17, :], wdx_d[:, :], csem2, sb_writes=[lnv[0:17, :]])
    copy("dve", identb[:], identf)
    memset(onesb[:], 1.0)
    copy("dve", wdec[0:17, :], lnv[0:17, :])
    nsink = st[:, 128:144]
    ts(nsink, sinks, -1.0, None, ALU.mult)
    memset(gzx, 1.0)

    NCV = 4
    cv = [[nc.alloc_semaphore("cv%d" % i), 0] for i in range(NCV)]
    cvn = [0]

    def conv(dst, src):
        i = cvn[0]
        cvn[0] += 1
        return S.dma("pool", dst, src, cv[i % NCV])

    gzt = conv(wgz_d.rearrange("p (kc c) -> p kc c", kc=16),
               w_d["w_in"].rearrange("(kc p) c -> p kc c", p=128)[:, :, 3072:3088])
    gsem = [nc.alloc_semaphore("gzl"), 0]
    S.dma("sp", wgz[:], wgz_d.rearrange("p (kc c) -> p kc c", kc=16), gsem, sb_writes=[wgz[:]], extra=[gzt])

    slab_ready = []
    for s, (name, kc, ncols, pcs) in enumerate(plan):
        toks = []
        dv = wsc[s][:, 0:kc * ncols].rearrange("p (kc c) -> p kc c", kc=kc)
        for (wn, r0, c0, n, d0) in pcs:
            src = w_d[wn].rearrange("(kc p) c -> p kc c", p=128)[:, r0:r0 + kc, c0:c0 + n]
            toks.append(conv(dv[:, :, d0:d0 + n], src))
        slab_ready.append(toks)

    rsem = [[nc.alloc_semaphore("ring%d" % i), 0] for i in range(RING)]
    ld = {"n": 0}
    total_slabs = NT * NS
    slab_views = {}

    def issue_slab():
        i = ld["n"]
        if i >= total_slabs:
            return
        ld["n"] += 1
        s = i % NS
        slot = i % RING
        extra = slab_ready[s] if i < NS else ()
        n_el = plan[s][1] * plan[s][2]
        S.dma("sp", ring[:, slot, 0:n_el], wsc[s][:, 0:n_el], rsem[slot], sb_writes=[ring[:, slot, 0:n_el]], extra=extra)

    use = {"n": 0}

    def next_slab(expect, issue=True):
        i = use["n"]
        use["n"] += 1
        s = i % NS
        name, kc, ncols, _ = plan[s]
        assert name == expect, (name, expect)
        if issue:
            issue_slab()
        return ring[:, i % RING, 0:kc * ncols].rearrange("p (kc c) -> p kc c", kc=kc)

    for _ in range(RING - 1):
        issue_slab()

    xsem = [[nc.alloc_semaphore("xs%d" % i), 0] for i in range(8)]
    psem = [nc.alloc_semaphore("pld"), 0]
    osem = [[nc.alloc_semaphore("ost%d" % i), 0] for i in range(2)]
    dbg_toks = []

    def dump(name, ap, dt):
        if not debug:
            return
        shp = list(ap.shape)
        d = nc.dram_tensor("dbg_" + name, shp, dt, kind="ExternalOutput").ap()
        sem = [nc.alloc_semaphore("dbg_" + name), 0]
        dbg_toks.append(S.dma("sp", d, ap, sem, sb_reads=[ap]))

    def load_x(t, m, hf):
        r0 = t * T + m * 128
        dst = xs[:, m, hf * 1024:(hf + 1) * 1024]
        S.dma("sp", dst, x_d[r0:r0 + 128, hf * 1024:(hf + 1) * 1024], xsem[m * 2 + hf], sb_writes=[dst])

    def norm(gcol, writer):
        sq_flush()
        act(lnv, ssq_bank[:, :], AF.Ln, bias=EPS, scale=1.0 / D)
        rb = bank()
        act(rb[:, :], lnv, AF.Exp, scale=-0.5)
        for k in range(16):
            writer(k, rb[:, :], gains[:, gcol + k:gcol + k + 1])

    def norm_to_uT(gcol):
        norm(gcol, lambda k, r, g: stt(uT[:, k, :], hT[:, k, :], g, r, ALU.mult, ALU.mult))

    ev = {"n": 0}

    def evac_eng():
        ev["n"] += 1
        return "act" if ev["n"] % 2 == 0 else "dve"

    def dense_b_pair(sl0, sl1, kc, rhs_of_k):
        pbs = [bank() for _ in range(4)]
        for k in range(kc):
            for c4 in range(4):
                sl = sl0 if c4 < 2 else sl1
                mm(pbs[c4][:, :], lhsT=sl[:, k, 128 * (c4 % 2):128 * (c4 % 2) + 128], rhs=rhs_of_k(k),
                   start=(k == 0), stop=(k == kc - 1), signal=(k == kc - 1))
        return pbs

    def dense_b(slab, kc, c0, rhs_of_k):
        pb = bank()
        for k in range(kc):
            mm(pb[:, :], lhsT=slab[:, k, c0:c0 + 128], rhs=rhs_of_k(k), start=(k == 0), stop=(k == kc - 1),
               signal=(k == kc - 1))
        return pb

    def s1_gen(t):
        first = (t % TPS == 0)
        S.stage = "S1"
        if t == 0:
            for m_ in range(4):
                for hf_ in range(2):
                    load_x(0, m_, hf_)
        S.dma("sp", pst, p_d[t * T:(t + 1) * T, :].rearrange("(m p) c -> p m c", p=128), psem, sb_writes=[pst])
        for m in range(4):
            for kg in range(4):
                S.stage = "S1"
                pb = bank()
                for j in range(4):
                    k = 4 * kg + j
                    tr(pb[:, j * 128:(j + 1) * 128], xs[:, m, k * 128:(k + 1) * 128], identf, signal=(j == 3))
                copy(evac_eng(), hT[:, 4 * kg:4 * kg + 4, m * 128:(m + 1) * 128],
                     pb[:, :].rearrange("p (a b) -> p a b", a=4))
                if m == 3:
                    for k in range(4 * kg, 4 * kg + 4):
                        sq_accum(k)
                yield
            S.stage = "S1"
            pb = bank()
            for j in range(2):
                tr(pb[:, j * 128:(j + 1) * 128], pst[:, m, j * 128:(j + 1) * 128], identf, signal=(j == 1))
            copy(evac_eng(), pT[:, :, m * 128:(m + 1) * 128], pb[:, 0:256].rearrange("p (a b) -> p a b", a=2))
        if first:
            memset(Sst[:], 0.0)
            memset(Sbf[:], 0.0)
        else:
            copy("dve", skx[:, :, 0:128], skx[:, :, 512:640])
            copy("dve", svx[:, 0, :], svx[:, 4, :])

    def s10_gen(t):
        for m in range(4):
            for kg in range(4):
                S.stage = "S10"
                pb = bank()
                for j in range(4):
                    k = 4 * kg + j
                    tr(pb[:, j * 128:(j + 1) * 128], oT[:, k, m * 128:(m + 1) * 128], identf, signal=(j == 3))
                copy(evac_eng(), osb[:, kg * 512:(kg + 1) * 512], pb[:, :])
                if kg % 2 == 1:
                    hf = kg // 2
                    r0 = t * T + m * 128
                    S.dma("sp", out_d[r0:r0 + 128, hf * 1024:(hf + 1) * 1024], osb[:, hf * 1024:(hf + 1) * 1024],
                          osem[hf], sb_reads=[osb[:, hf * 1024:(hf + 1) * 1024]])
                yield

    for _ in s1_gen(0):
        pass
    for t in range(NT):
        first = (t % TPS == 0)
        norm_to_uT(0)

        S.stage = "S2"
        sl0 = next_slab("gq0")
        sl1 = next_slab("gq1", issue=False)
        pbs = dense_b_pair(sl0, sl1, 16, lambda k: uT[:, k, :])
        issue_slab()
        for c4 in range(4):
            copy(evac_eng(), qT[:, c4, :], pbs[c4][:, :])
        for j in range(2):
            sl = next_slab("gk%d" % j)
            for c in range(2):
                pb = dense_b(sl, 16, 128 * c, lambda k: uT[:, k, :])
                copy(evac_eng(), kT[:, 2 * j + c, :], pb[:, :])

        def dense_a(sl, m, half, pb):
            for k in range(16):
                mm(pb[:, half * 256:(half + 1) * 256], lhsT=uT[:, k, m * 128:(m + 1) * 128], rhs=sl[:, k, :],
                   start=(k == 0), stop=(k == 15), signal=(k == 15))

        for j in range(4):
            sl = next_slab("gv%d" % j)
            for mp in range(2):
                pb = bank()
                for hf in range(2):
                    dense_a(sl, 2 * mp + hf, hf, pb)
                copy(evac_eng(), vt[:, 2 * mp:2 * mp + 2, 256 * j:256 * (j + 1)],
                     pb[:, :].rearrange("p (a b) -> p a b", a=2))
        for j in range(4):
            sl = next_slab("gr%d" % j)
            for mp in range(2):
                pb = bank()
                for hf in range(2):
                    dense_a(sl, 2 * mp + hf, hf, pb)
                act(fA[:, 0, :], pb[:, :], AF.Tanh, scale=0.5)
                stt(fA[:, 1, :], fA[:, 0, :], 1.0, pb[:, :], ALU.add, ALU.mult)
                gsl = gnb[:, (256 * j) % 256:(256 * j) % 256 + 256]
                for hf in range(2):
                    tt(G2[:, 2 * mp + hf, 256 * j:256 * (j + 1)], fA[:, 1, hf * 256:(hf + 1) * 256], gsl, ALU.mult)
        pb = bank()
        for k in range(16):
            mm(pb[0:16, :], lhsT=wgz[:, k, :], rhs=uT[:, k, :], start=(k == 0), stop=(k == 15), signal=(k == 15))
        copy("dve", gzx[0:16, :], pb[0:16, :])
        S.stage = "S3"
        def dense_swa_gen():
            for i in range(4):
                sl = next_slab("sq%d" % i)
                for c in range(2):
                    pb = dense_b(sl, 16, 128 * c, lambda k: uT[:, k, :])
                    copy(evac_eng(), sqT[:, 2 * i + c, :], pb[:, :])
                    yield
            sl = next_slab("sk")
            for c in range(2):
                pb = dense_b(sl, 16, 128 * c, lambda k: uT[:, k, :])
                copy(evac_eng(), skx[:, c, 128:640], pb[:, :])
                yield
            sl = next_slab("sv")
            for mp in range(2):
                pb = bank()
                for hf in range(2):
                    dense_a(sl, 2 * mp + hf, hf, pb)
                copy(evac_eng(), svx[:, 1 + 2 * mp:3 + 2 * mp, :], pb[:, :].rearrange("p (a b) -> p a b", a=2))
                yield

        def gla_gen():
            for m in range(4):
                tk = slice(m * 128, (m + 1) * 128)
                pz = bank()
                mm(pz[:, :], lhsT=gzx[0:17, tk], rhs=wdec[0:17, :])
                act(gE2, pz[:, :], AF.Exp, scale=-1.0)
                act(gL, gE2, AF.Ln, bias=1.0)
                yield
                pbt = bank()
                for h in range(4):
                    mm(pbt[:, h * 128:(h + 1) * 128], lhsT=gL[:, h * 128:(h + 1) * 128], rhs=trif, signal=(h == 3))
                act(gE1, pbt[:, :], AF.Exp, scale=-1.0 / 16.0)
                act(gE2, pbt[:, :], AF.Exp, scale=1.0 / 16.0)
                yield
                stt(qtl, qT[:, :, tk], 128.0 ** -0.5, gE1.rearrange("p (a b) -> p a b", a=4), ALU.mult, ALU.mult)
                tt(ktl, kT[:, :, tk], gE2.rearrange("p (a b) -> p a b", a=4), ALU.mult)
                yield
                pk = bank()
                pkb = pk[:, 0:256].bitcast(BF16)
                for h in range(4):
                    tr(pkb[:, h * 128:(h + 1) * 128], ktl[:, h, :], identb[:], signal=(h == 3))
                copy("act", ktm, pkb)
                pa = bank()
                for h in range(4):
                    mm(pa[:, h * 128:(h + 1) * 128], lhsT=ktl[:, h, :], rhs=qtl[:, h, :], signal=(h == 3))
                tt(gAT, pa[:, :].rearrange("p (a b) -> p a b", a=4), tri4.rearrange("p (a b) -> p a b", a=4), ALU.mult)
                yield
                po = [bank(), bank()]
                for h in range(4):
                    o = po[h // 2][:, (h % 2) * 256:(h % 2 + 1) * 256]
                    mm(o, lhsT=gAT[:, h, :], rhs=vt[:, m, h * 256:(h + 1) * 256], start=True, stop=False, signal=False)
                    mm(o, lhsT=qtl[:, h, :], rhs=Sbf[:, h, :], start=False, stop=True, signal=(h % 2 == 1))
                pd = [bank(), bank()]
                for h in range(4):
                    mm(pd[h // 2][:, (h % 2) * 256:(h % 2 + 1) * 256], lhsT=ktm[:, h * 128:(h + 1) * 128],
                       rhs=vt[:, m, h * 256:(h + 1) * 256], signal=(h % 2 == 1))
                for h in range(4):
                    o = po[h // 2][:, (h % 2) * 256:(h % 2 + 1) * 256]
                    act(fA[:, 2, 0:256], o, AF.Square, accum=st[:, h:h + 1])
                act(st[:, 4:8], st[:, 0:4], AF.Ln, bias=EPS, scale=1.0 / 256.0)
                act(st[:, 8:12], st[:, 4:8], AF.Exp, scale=-0.5, bias=math.log(0.5))
                for h in range(4):
                    o = po[h // 2][:, (h % 2) * 256:(h % 2 + 1) * 256]
                    stt(yat[:, h * 256:(h + 1) * 256], o, st[:, 8 + h:9 + h], G2[:, m, h * 256:(h + 1) * 256],
                        ALU.mult, ALU.mult)
                yield
                for i2 in range(2):
                    tt(Sst[:, 2 * i2:2 * i2 + 2, :], Sst[:, 2 * i2:2 * i2 + 2, :],
                       pd[i2][:, :].rearrange("p (a b) -> p a b", a=2), ALU.add)
                dl = gE1[:, 127:128]
                dec = bass.AP(dl.tensor, dl.offset, [list(dl.ap[0]), [128, 4], [0, 256]])
                tt(Sst[:], Sst[:], dec, ALU.mult)
                copy("act", Sbf[:], Sst[:])
                yield
                pt_ = bank()
                ptb = pt_[:, :].bitcast(BF16)
                for c in range(8):
                    tr(ptb[:, c * 128:(c + 1) * 128], yat[:, c * 128:(c + 1) * 128], identb[:], signal=(c == 7))
                copy("act", yaT[:, :, tk], ptb.rearrange("p (a b) -> p a b", a=8))
                yield

        gg = gla_gen()
        gla_live = [True]

        def gla_steps(n):
            for _ in range(n):
                if gla_live[0]:
                    try:
                        next(gg)
                    except StopIteration:
                        gla_live[0] = False

        for _ in dense_swa_gen():
            gla_steps(2)

        def swa_P1(u):
            m, kvp = u // 2, u % 2
            tk = slice(m * 128, (m + 1) * 128)
            k0 = 128 if (first and m == 0) else 0
            b = u % 2
            so = 16 + 56 * kvp
            ps_ = [bank() for _ in range(4)]
            for jj in (0, 4, 1, 5, 2, 6, 3, 7):
                hf, j = jj // 4, jj % 4
                prow = slice(hf * 64, (hf + 1) * 64)
                c = kvp * 4 + j
                mm(ps_[jj // 2][:, (jj % 2) * 256 + k0:(jj % 2 + 1) * 256], lhsT=sqT[prow, c, tk],
                   rhs=skx[prow, kvp, m * 128 + k0:m * 128 + 256], signal=True)
            for jj in range(8):
                h = 8 * kvp + jj
                stt(zz[b][:, jj, k0:256], dist8[:, k0:256], -SLOPES[h],
                    ps_[jj // 2][:, (jj % 2) * 256 + k0:(jj % 2 + 1) * 256], ALU.mult, ALU.add)
            S.op("dve", lambda g, o=st[:, so:so + 8], i=zz[b][:, :, k0:256]: g.tensor_reduce(out=o, in_=i, axis=AX.X, op=ALU.max),
                 reads=[zz[b][:, :, k0:256]], writes=[st[:, so:so + 8]])

        def swa_P1b(u):
            m, kvp = u // 2, u % 2
            k0 = 128 if (first and m == 0) else 0
            b = u % 2
            so = 16 + 56 * kvp
            stt(st[:, so + 8:so + 16], st[:, so:so + 8], -0.125, nsink[:, 8 * kvp:8 * kvp + 8], ALU.mult, ALU.min)
            tt(st[:, so + 16:so + 24], sinks[:, 8 * kvp:8 * kvp + 8], st[:, so + 8:so + 16], ALU.add)
            act(st[:, so + 24:so + 32], st[:, so + 16:so + 24], AF.Exp)
            for jj in range(8):
                act(Pb[b][:, jj, k0:256], zz[b][:, jj, k0:256], AF.Exp, bias=st[:, so + 8 + jj:so + 9 + jj], scale=0.125,
                    accum=st[:, so + 32 + jj:so + 33 + jj])

        def swa_P2(u):
            m, kvp = u // 2, u % 2
            k0 = 128 if (first and m == 0) else 0
            b = u % 2
            so = 16 + 56 * kvp
            tt(st[:, so + 40:so + 48], st[:, so + 24:so + 32], st[:, so + 32:so + 40], ALU.add)
            S.op("dve", lambda g, o=st[:, so + 48:so + 56], i=st[:, so + 40:so + 48]: g.reciprocal(out=o, in_=i),
                 reads=[st[:, so + 40:so + 48]], writes=[st[:, so + 48:so + 56]])
            halves = [1] if k0 else [0, 1]
            n_tr = 4 * len(halves)
            for i2 in range(2):
                pp_ = bank()
                ppb = pp_[:, :].bitcast(BF16)
                cnt = 0
                for j in range(4):
                    jj = 4 * i2 + j
                    for hh in halves:
                        cnt += 1
                        tr(ppb[:, (2 * j + hh) * 128:(2 * j + hh + 1) * 128], Pb[b][:, jj, hh * 128:(hh + 1) * 128],
                           identb[:], signal=(cnt == n_tr))
                dst = PTb[b][:, 8 * i2:8 * i2 + 8, :]
                eng = "act"
                if k0:
                    copy(eng, dst.rearrange("p (j h) q -> p j h q", h=2)[:, :, 1, :],
                         ppb.rearrange("p (j h q) -> p j h q", j=4, h=2)[:, :, 1, :])
                else:
                    copy(eng, dst, ppb.rearrange("p (a b) -> p a b", a=8))

        def swa_P3(u):
            m, kvp = u // 2, u % 2
            tk = slice(m * 128, (m + 1) * 128)
            k0 = 128 if (first and m == 0) else 0
            b = u % 2
            so = 16 + 56 * kvp
            halves = [1] if k0 else [0, 1]
            py = bank()
            for jj in range(8):
                kv = 2 * kvp + jj // 4
                o = py[:, jj * 64:(jj + 1) * 64]
                for ii, hh in enumerate(halves):
                    mm(o, lhsT=PTb[b][:, 2 * jj + hh, :], rhs=svx[:, m + hh, kv * 64:(kv + 1) * 64],
                       start=(ii == 0), stop=(ii == len(halves) - 1),
                       signal=(ii == len(halves) - 1 and jj == 7))
            yb_ = ybn[m % 2]
            tt(yb_[:, 8 * kvp * 64:(8 * kvp + 8) * 64].rearrange("p (a b) -> p a b", a=8),
               py[:, :].rearrange("p (a b) -> p a b", a=8),
               st[:, so + 48:so + 56].unsqueeze(2).broadcast_to([128, 8, 64]), ALU.mult)
            if kvp == 1:
                pt_ = bank()
                ptb = pt_[:, :].bitcast(BF16)
                for c in range(8):
                    tr(ptb[:, c * 128:(c + 1) * 128], yb_[:, c * 128:(c + 1) * 128], identb[:], signal=(c == 7))
                copy("dve", ybT[:, :, tk], ptb.rearrange("p (a b) -> p a b", a=8))

        for s_ in range(8 + 3):
            if s_ < 8:
                swa_P1(s_)
            if 0 <= s_ - 2 < 8:
                swa_P2(s_ - 2)
            if 0 <= s_ - 3 < 8:
                swa_P3(s_ - 3)
            if s_ < 8:
                swa_P1b(s_)
            gla_steps(2)
        gla_steps(100)

        if t == 0:
            dump("uT", uT[:], BF16)
            dump("yaT", yaT, BF16)
            dump("ybT", ybT, BF16)
            dump("qT", qT, BF16)
            dump("vt", vt, BF16)
            dump("G2", G2, BF16)
            dump("sqT", sqT, BF16)
            dump("skx", skx, BF16)
            dump("svx", svx, BF16)
        S.stage = "S4"
        for j in range(8):
            slA = next_slab("A%d" % j)
            pA = [dense_b(slA, 8, 128 * c, lambda k: yaT[:, k, :]) for c in range(2)]
            sga = next_slab("ga%d" % j)
            for c in range(2):
                pg = dense_b(sga, 16, 128 * c, lambda k: uT[:, k, :])
                act(fA[:, 0, :], pg[:, :], AF.Tanh, scale=0.5)
                stt(fA[:, 1 + c, :], fA[:, 0, :], 1.0, pA[c][:, :], ALU.add, ALU.mult)
            slB = next_slab("B%d" % j)
            pB = [dense_b(slB, 8, 128 * c, lambda k: ybT[:, k, :]) for c in range(2)]
            sgb = next_slab("gb%d" % j)
            for c in range(2):
                f = 2 * j + c
                pg = dense_b(sgb, 16, 128 * c, lambda k: uT[:, k, :])
                act(fA[:, 0, :], pg[:, :], AF.Tanh, scale=0.5)
                stt(fA[:, 0, :], fA[:, 0, :], 1.0, pB[c][:, :], ALU.add, ALU.mult)
                tt(mgT[:, f, :], fA[:, 0, :], fA[:, 1 + c, :], ALU.add)

        if t == 0:
            dump("mgT", mgT, BF16)
        S.stage = "S5"
        for j in range(8):
            sl = next_slab("wo%d" % j)
            for c in range(2):
                f = 2 * j + c
                pb = dense_b(sl, 16, 128 * c, lambda k: mgT[:, k, :])
                stt(hT[:, f, :], pb[:, :], 0.5, hT[:, f, :], ALU.mult, ALU.add)
                sq_accum(f)

        if t == 0:
            dump("h1", hT[:], F32)
        S.stage = "S6-8"
        norm_to_uT(16)
        if t + 1 < NT:
            for m_ in range(4):
                for hf_ in range(2):
                    load_x(t + 1, m_, hf_)
        for q in range(4):
            ab = actb[q % 2]
            j_start = 0
            if q == 0:
                sl0 = next_slab("up0_0")
                sl1 = next_slab("up0_1", issue=False)
                pbs = dense_b_pair(sl0, sl1, 16, lambda k: uT[:, k, :])
                issue_slab()
                for c4 in range(4):
                    act(fA[:, c4 % 3, :], pbs[c4][:, :], AF.Square)
                    stt(ab[:, c4, :], pbs[c4][:, :], 0.0, fA[:, c4 % 3, :], ALU.is_gt, ALU.mult)
                j_start = 2
            for j in range(j_start, 8):
                sl = next_slab("up%d_%d" % (q, j))
                for c in range(2):
                    pb = dense_b(sl, 16, 128 * c, lambda k: uT[:, k, :])
                    act(fA[:, (2 * j + c) % 3, :], pb[:, :], AF.Square)
                    stt(ab[:, 2 * j + c, :], pb[:, :], 0.0, fA[:, (2 * j + c) % 3, :], ALU.is_gt, ALU.mult)
            for j in range(8):
                sl = next_slab("dn%d_%d" % (q, j))
                for c in range(2):
                    f = 2 * j + c
                    pb = dense_b(sl, 16, 128 * c, lambda k: ab[:, k, :])
                    tt(hT[:, f, :], pb[:, :], hT[:, f, :], ALU.add)
                    if q == 3:
                        sq_accum(f)

        if t == 0:
            dump("h2", hT[:], F32)
        S.stage = "S9"
        norm_to_uT(32)
        for j in range(8):
            slp = next_slab("pp%d" % j)
            ppb_ = [dense_b(slp, 2, 128 * c, lambda k: pT[:, k, :]) for c in range(2)]
            sl = next_slab("pg%d" % j)
            for c in range(2):
                f = 2 * j + c
                pp_ = ppb_[c]
                pg = dense_b(sl, 16, 128 * c, lambda k: uT[:, k, :])
                act(fA[:, c, :], pg[:, :], AF.Tanh, scale=0.5)
                stt(fA[:, c, :], fA[:, c, :], 1.0, pp_[:, :], ALU.add, ALU.mult)
                stt(hT[:, f, :], fA[:, c, :], 0.5, hT[:, f, :], ALU.mult, ALU.add)
                sq_accum(f)

        if t == 0:
            dump("h3", hT[:], F32)
        S.stage = "S10"
        norm(48, lambda k, r, g: stt(oT[:, k, :], hT[:, k, :], g, r, ALU.mult, ALU.mult))
        g10 = s10_gen(t)
        g1 = s1_gen(t + 1) if t + 1 < NT else iter(())
        live = [True, True]
        while live[0] or live[1]:
            for gi, gg_ in enumerate((g10, g1)):
                if live[gi]:
                    try:
                        next(gg_)
                    except StopIteration:
                        live[gi] = False

    S.wait_only("sp", [(o_[0], o_[1]) for o_ in osem] + dbg_toks)
    for en in ("pe", "act", "dve"):
        assert not getattr(S.eng[en], "pending", False), en

    nc_sched_holder.clear()
    nc_sched_holder.append(S)
    with nc.Block() as block:
        @block.sync
        def _(g):
            S.emit("sp", g)

        @block.gpsimd
        def _(g):
            S.emit("pool", g)

        @block.tensor
        def _(g):
            S.emit("pe", g)

        @block.scalar
        def _(g):
            S.emit("act", g)

        @block.vector
        def _(g):
            S.emit("dve", g)
    return nc


def make_consts(norm_mix, norm_mlp, norm_ple, norm_final, attn_sinks, gla_norm):
    c = np.zeros((128, C_END), np.float32)
    for n, g in enumerate((norm_mix, norm_mlp, norm_ple, norm_final)):
        c[:, C_GAIN + 16 * n:C_GAIN + 16 * (n + 1)] = np.asarray(g, np.float32).reshape(16, 128).T
    c[:, C_SINK:C_SINK + 16] = np.asarray(attn_sinks, np.float32).reshape(1, 16)
    c[:, C_GN:C_GN + 256] = np.asarray(gla_norm, np.float32).reshape(1, 256)
    c[:, C_ID:C_ID + 128] = np.eye(128, dtype=np.float32)
    tri = (np.arange(128)[:, None] <= np.arange(128)[None, :]).astype(np.float32)
    c[:, C_TRI:C_TRI + 512] = np.tile(tri, (1, 4))
    i = np.arange(128)[:, None]
    j = np.arange(256)[None, :]
    dist = i + 128 - j
    ok = (dist >= 0) & (dist < 128)
    c[:, C_DIST:C_DIST + 256] = np.where(ok, 8.0 * dist, 1e9).astype(np.float32)
    return c


_NC_CACHE = {}


def kernel(x, p, norm_mix, w_in, w_decay, b_decay, gla_norm, attn_sinks, w_branch_a, w_branch_b,
           w_out, norm_mlp, w_up, w_down, norm_ple, w_ple_gate, w_ple_proj, norm_final):
    x = np.asarray(x, np.float32)
    B, SEQ, _ = x.shape
    n_cores = N_CORES
    per = B // n_cores
    key = (per, SEQ)
    if key not in _NC_CACHE:
        _NC_CACHE[key] = build(n_seq=per, seq_len=SEQ)
    nc = _NC_CACHE[key]
    cst = make_consts(np.asarray(norm_mix)[0], np.asarray(norm_mlp)[0], np.asarray(norm_ple)[0],
                      np.asarray(norm_final), np.asarray(attn_sinks)[0], np.asarray(gla_norm)[0])
    wdx = np.concatenate([np.asarray(w_decay, np.float32)[0], np.asarray(b_decay, np.float32)[0][None, :]], axis=0)
    wts = {
        "w_in": np.ascontiguousarray(np.asarray(w_in, np.float32)[0]),
        "w_branch_a": np.ascontiguousarray(np.asarray(w_branch_a, np.float32)[0]),
        "w_branch_b": np.ascontiguousarray(np.asarray(w_branch_b, np.float32)[0]),
        "w_out": np.ascontiguousarray(np.asarray(w_out, np.float32)[0]),
        "w_up": np.ascontiguousarray(np.asarray(w_up, np.float32)[0]),
        "w_down": np.ascontiguousarray(np.asarray(w_down, np.float32)[0]),
        "w_ple_gate": np.ascontiguousarray(np.asarray(w_ple_gate, np.float32)[0]),
        "w_ple_proj": np.ascontiguousarray(np.asarray(w_ple_proj, np.float32)[0]),
    }
    pp = np.asarray(p, np.float32)[0]
    in_maps = []
    for c in range(n_cores):
        m = {"x": np.ascontiguousarray(x[c * per:(c + 1) * per].reshape(per * SEQ, D)),
             "p": np.ascontiguousarray(pp[c * per:(c + 1) * per].reshape(per * SEQ, 256)),
             "cst": cst, "wdx": wdx}
        m.update(wts)
        in_maps.append(m)
    res = run_bass_kernel_spmd(nc, in_maps, core_ids=list(range(n_cores)))
    out = np.concatenate([np.asarray(r["out"]).reshape(per, SEQ, D) for r in res.results], axis=0)
    return out.astype(np.float32)
```

```python
import math
import numpy as np
import concourse.bass as bass
import concourse.mybir as mybir
from concourse.bass_utils import run_bass_kernel_spmd

F32 = mybir.dt.float32
BF16 = mybir.dt.bfloat16
AF = mybir.ActivationFunctionType
ALU = mybir.AluOpType
AX = mybir.AxisListType

D = 2048
T = 512
EPS = 1e-6
N_CORES = 8
SLOPES = [2.0 ** (-8.0 * (h + 1) / 16.0) for h in range(16)]

C_GAIN, C_SINK, C_GN, C_ID, C_TRI, C_DIST, C_END = 0, 64, 80, 336, 464, 976, 1232

A_QT, A_KT, A_V, A_G2, A_SQT, A_YAT, A_YBT = 0, 4096, 8192, 16384, 24576, 32768, 40960
A_SKX, A_SVX, A_GZX = 49152, 51712, 54272
A_GS = 55296
A_SS = 67584
A_END = 67584 + 2 * 16384 + 2 * 2048
RING = 3
SAME_ENG_WAR = False


def _esz(dt):
    return 2 if dt == BF16 else 4


def _iv(ap):
    pat = ap.ap
    row = pat[0][0]
    off = ap.offset
    lo = off % row if row > 0 else off
    ext = 1
    for s, c in pat[1:]:
        ext += (c - 1) * abs(s)
    e = _esz(ap.dtype)
    name = ap.tensor.name
    if name.startswith("bank"):
        return (name, 0, 2048)
    return (name, lo * e, (lo + ext) * e)


class _Eng:
    def __init__(self, name, sem):
        self.name = name
        self.sem = sem
        self.count = 0
        self.waited = {}
        self.ops = []


class Sched:
    def __init__(self, nc):
        self.nc = nc
        self.eng = {}
        for n in ("pe", "act", "dve", "pool", "sp"):
            self.eng[n] = _Eng(n, nc.alloc_semaphore("prog_" + n))
        self.segs = {}
        self.semobj = {}
        self.stage = ""

    def _split(self, key, x):
        L = self.segs.setdefault(key, [])
        for i, s in enumerate(L):
            if s[0] < x < s[1]:
                L.insert(i + 1, [x, s[1], s[2], dict(s[3])])
                s[1] = x
                return

    def _cover(self, key, lo, hi):
        self._split(key, lo)
        self._split(key, hi)
        return [s for s in self.segs.setdefault(key, []) if s[0] >= lo and s[1] <= hi]

    def _deps(self, reads, writes):
        toks = []
        for (key, lo, hi) in reads:
            for s in self._cover(key, lo, hi):
                if s[2] is not None:
                    toks.append((s[2], "w"))
                if key.startswith("bank"):
                    for sem, v in s[3].items():
                        toks.append(((sem, v), "r"))
        for (key, lo, hi) in writes:
            for s in self._cover(key, lo, hi):
                if s[2] is not None:
                    toks.append((s[2], "w"))
                for sem, v in s[3].items():
                    toks.append(((sem, v), "r"))
        return toks

    def _commit(self, tok, reads, writes):
        for (key, lo, hi) in reads:
            for s in self._cover(key, lo, hi):
                if s[3].get(tok[0], 0) < tok[1]:
                    s[3][tok[0]] = tok[1]
            self._fill(key, lo, hi, None, tok)
        for (key, lo, hi) in writes:
            L = self.segs.setdefault(key, [])
            self._split(key, lo)
            self._split(key, hi)
            L[:] = [s for s in L if not (s[0] >= lo and s[1] <= hi)]
            L.append([lo, hi, tok, {}])
            L.sort(key=lambda s: s[0])

    def _fill(self, key, lo, hi, w, rtok):
        L = self.segs.setdefault(key, [])
        cur = lo
        new = []
        for s in sorted(L, key=lambda s: s[0]):
            if s[1] <= lo or s[0] >= hi:
                continue
            if s[0] > cur:
                new.append([cur, s[0], w, {rtok[0]: rtok[1]}])
            cur = max(cur, s[1])
        if cur < hi:
            new.append([cur, hi, w, {rtok[0]: rtok[1]}])
        if new:
            L.extend(new)
            L.sort(key=lambda s: s[0])

    def _waits(self, e, toks):
        need = {}
        for (sem, v), kind in toks:
            if sem is e.sem:
                if e.name in ("pe", "sp") or (kind == "r" and not SAME_ENG_WAR):
                    continue
            if v > need.get(sem, 0):
                need[sem] = v
        out = []
        for sem, v in need.items():
            if e.waited.get(sem, 0) < v:
                e.waited[sem] = v
                out.append((sem, v))
        return out

    def op(self, en, fn, reads=(), writes=(), signal=True, extra=()):
        e = self.eng[en]
        r = [_iv(a) for a in reads]
        w = [_iv(a) for a in writes]
        toks = self._deps(r, w) + [(t, "w") for t in extra]
        waits = self._waits(e, toks)
        tok = (e.sem, e.count + 1)
        if signal:
            e.count += 1
        e.pending = not signal
        e.ops.append((waits, fn, (e.sem, 1) if signal else None, self.stage))
        self._commit(tok, r, w)
        return tok

    def dma(self, en, out, in_, dsem, sb_reads=(), sb_writes=(), extra=()):
        e = self.eng[en]
        r = [_iv(a) for a in sb_reads]
        w = [_iv(a) for a in sb_writes]
        toks = self._deps(r, w) + [(t, "w") for t in extra]
        if dsem[1] > 0:
            toks.append(((dsem[0], dsem[1]), "w"))
        waits = self._waits(e, toks)
        dsem[1] += 16
        tok = (dsem[0], dsem[1])
        e.ops.append((waits, (lambda g, o=out, i=in_: g.dma_start(out=o, in_=i)), (dsem[0], 16), self.stage))
        self._commit(tok, r, w)
        return tok

    def wait_only(self, en, toks):
        e = self.eng[en]
        waits = self._waits(e, [(t, "w") for t in toks])
        if waits:
            e.ops.append((waits, None, None, self.stage))

    def emit(self, en, g):
        for waits, fn, inc, _lab in self.eng[en].ops:
            for sem, v in waits:
                g.wait_ge(sem, v)
            if fn is None:
                continue
            ins = fn(g)
            if inc is not None:
                ins.then_inc(inc[0], inc[1])


def _slab_plan():
    P = []

    def simple(name, w, c0, kc=16, ncols=256, r0=0):
        P.append((name, kc, ncols, [(w, r0, c0, ncols, 0)]))

    for j in range(2):
        simple("gq%d" % j, "w_in", 256 * j)
    for j in range(2):
        simple("gk%d" % j, "w_in", 512 + 256 * j)
    for j in range(4):
        simple("gv%d" % j, "w_in", 1024 + 256 * j)
    for j in range(4):
        simple("gr%d" % j, "w_in", 2048 + 256 * j)
    for i in range(4):
        pcs = []
        for cc in range(2):
            c = 2 * i + cc
            ha = c if c < 4 else 8 + (c - 4)
            hb = ha + 4
            pcs.append(("w_in", 0, 3088 + 64 * ha, 64, 128 * cc))
            pcs.append(("w_in", 0, 3088 + 64 * hb, 64, 128 * cc + 64))
        P.append(("sq%d" % i, 16, 256, pcs))
    simple("sk", "w_in", 4112)
    simple("sv", "w_in", 4368)
    for j in range(8):
        simple("A%d" % j, "w_branch_a", 256 * j, kc=8, ncols=256)
        simple("ga%d" % j, "w_in", 4624 + 256 * j)
        simple("B%d" % j, "w_branch_b", 256 * j, kc=8, ncols=256)
        simple("gb%d" % j, "w_in", 6672 + 256 * j)
    for j in range(8):
        simple("wo%d" % j, "w_out", 256 * j)
    for q in range(4):
        for j in range(8):
            simple("up%d_%d" % (q, j), "w_up", 2048 * q + 256 * j)
        for j in range(8):
            simple("dn%d_%d" % (q, j), "w_down", 256 * j, r0=16 * q)
    for j in range(8):
        simple("pp%d" % j, "w_ple_proj", 256 * j, kc=2, ncols=256)
        simple("pg%d" % j, "w_ple_gate", 256 * j)
    return P


W_SHAPES = {
    "w_in": (2048, 8720), "w_branch_a": (1024, 2048), "w_branch_b": (1024, 2048),
    "w_out": (2048, 2048), "w_up": (2048, 8192), "w_down": (8192, 2048),
    "w_ple_gate": (2048, 2048), "w_ple_proj": (256, 2048),
}


nc_sched_holder = []


def build(n_seq=2, seq_len=2048, debug=None):
    NTOK = n_seq * seq_len
    NT = NTOK // T
    TPS = seq_len // T
    nc = bass.Bass("TRN2", target_bir_lowering=False)
    x_d = nc.dram_tensor("x", [NTOK, D], F32, kind="ExternalInput").ap()
    p_d = nc.dram_tensor("p", [NTOK, 256], F32, kind="ExternalInput").ap()
    cst_d = nc.dram_tensor("cst", [128, C_END], F32, kind="ExternalInput").ap()
    wdx_d = nc.dram_tensor("wdx", [17, 512], F32, kind="ExternalInput").ap()
    w_d = {n: nc.dram_tensor(n, list(s), F32, kind="ExternalInput").ap() for n, s in W_SHAPES.items()}
    out_d = nc.dram_tensor("out", [NTOK, D], F32, kind="ExternalOutput").ap()
    plan = _slab_plan()
    NS = len(plan)
    wsc = nc.dram_tensor("wsc", [NS, 128, 4096], BF16).ap()
    wgz_d = nc.dram_tensor("wgzsc", [128, 256], BF16).ap()

    S = Sched(nc)
    A = nc.alloc_sbuf_tensor
    hT = A("hT", [128, 16, T], F32)
    uT = A("uT", [128, 16, T], BF16)
    ring = A("ring", [128, RING, 4096], BF16)
    cst = A("cst_sb", [128, C_END], F32)
    identb = A("identb", [128, 128], BF16)
    onesb = A("onesb", [128, 128], BF16)
    wdec = A("wdec", [32, 512], BF16)
    wgz = A("wgz", [128, 16, 16], BF16)
    Sst = A("Sst", [128, 4, 256], F32)
    Sbf = A("Sbf", [128, 4, 256], BF16)
    pT = A("pT", [128, 2, T], BF16)
    sqs = A("sqs", [128, 4, T], BF16)
    fA = A("fA", [128, 3, T], F32)
    st = A("stats", [128, 144], F32)
    lnv = fA[:, 2, :]
    arena = A("arena", [128, A_END // 2], BF16)
    banks = [nc.alloc_psum_tensor("bank%d" % i, [128, 512], F32) for i in range(8)]

    def av(off, nbytes, dt, pat=None, parts=128, **kw):
        a = arena[0:parts, off // 2:(off + nbytes) // 2]
        if dt != BF16:
            a = a.bitcast(dt)
        if pat:
            a = a.rearrange(pat, **kw)
        return a

    qT = av(A_QT, 4096, BF16, "p (a b) -> p a b", a=4)
    kT = av(A_KT, 4096, BF16, "p (a b) -> p a b", a=4)
    vt = av(A_V, 8192, BF16, "p (a b) -> p a b", a=4)
    mgT = av(A_QT, 16384, BF16, "p (a b) -> p a b", a=16)
    G2 = av(A_G2, 8192, BF16, "p (a b) -> p a b", a=4)
    sqT = av(A_SQT, 8192, BF16, "p (a b) -> p a b", a=8)
    yaT = av(A_YAT, 8192, BF16, "p (a b) -> p a b", a=8)
    ybT = av(A_YBT, 8192, BF16, "p (a b) -> p a b", a=8)
    skx = av(A_SKX, 2560, BF16, "p (a b) -> p a b", a=2)
    svx = av(A_SVX, 2560, BF16, "p (a b) -> p a b", a=5)
    gzx = av(A_GZX, 1024, BF16, parts=32)
    gL = av(A_GS, 2048, F32)
    gE1 = av(A_GS + 2048, 2048, F32)
    gE2 = av(A_GS + 4096, 2048, F32)
    qtl = av(A_GS + 6144, 1024, BF16, "p (a b) -> p a b", a=4)
    ktl = av(A_GS + 7168, 1024, BF16, "p (a b) -> p a b", a=4)
    ktm = av(A_GS + 8192, 1024, BF16)
    gAT = av(A_GS + 9216, 1024, BF16, "p (a b) -> p a b", a=4)
    yat = av(A_GS + 10240, 2048, BF16)
    xs = av(A_GS, 32768, F32, "p (a b) -> p a b", a=4)
    pst = av(A_GS + 32768, 4096, F32, "p (a b) -> p a b", a=4)
    osb = av(A_GS + 36864, 8192, F32)
    zz = [av(A_SS + 16384 * i, 8192, F32, "p (a b) -> p a b", a=8) for i in range(2)]
    Pb = [av(A_SS + 16384 * i + 8192, 4096, BF16, "p (a b) -> p a b", a=8) for i in range(2)]
    PTb = [av(A_SS + 16384 * i + 12288, 4096, BF16, "p (a b) -> p a b", a=16) for i in range(2)]
    ybn = [av(A_SS + 32768 + 2048 * i, 2048, BF16) for i in range(2)]
    actb = [av(16384 * i, 16384, BF16, "p (a b) -> p a b", a=16) for i in range(2)]
    oT = av(0, 32768, F32, "p (a b) -> p a b", a=16)

    gains = cst[:, C_GAIN:C_GAIN + 64]
    sinks = cst[:, C_SINK:C_SINK + 16]
    gnb = cst[:, C_GN:C_GN + 256]
    identf = cst[:, C_ID:C_ID + 128]
    tri4 = cst[:, C_TRI:C_TRI + 512]
    trif = cst[:, C_TRI:C_TRI + 128]
    dist8 = cst[:, C_DIST:C_DIST + 256]

    bank_i = [0]

    def bank():
        b = banks[bank_i[0] % 7]
        bank_i[0] += 1
        return b

    ssq_bank = banks[7]

    sq_pending = []

    def sq_accum(k, delay=3):
        act(sqs[:, k % 4, :], hT[:, k, :], AF.Square)
        sq_pending.append(k)
        while len(sq_pending) > delay:
            sq_flush(1)

    def sq_flush(n=100):
        while sq_pending and n > 0:
            k = sq_pending.pop(0)
            n -= 1
            mm(ssq_bank[:, :], lhsT=onesb[:], rhs=sqs[:, k % 4, :], start=(k == 0), stop=(k == 15), signal=True)

    def mm(out, lhsT, rhs, start=True, stop=True, signal=True):
        return S.op("pe", lambda g: g.matmul(out, lhsT=lhsT, rhs=rhs, start=start, stop=stop),
                    reads=[lhsT, rhs], writes=[out], signal=signal)

    def tr(out, in_, ident, signal=True):
        return S.op("pe", lambda g: g.transpose(out, in_, ident), reads=[in_, ident], writes=[out], signal=signal)

    def act(out, in_, func, bias=None, scale=None, accum=None, eng="act"):
        kw = {}
        rd = [in_]
        wr = [out]
        if bias is not None:
            kw["bias"] = bias
            if not isinstance(bias, (int, float)):
                rd.append(bias)
        if scale is not None:
            kw["scale"] = scale
            if not isinstance(scale, (int, float)):
                rd.append(scale)
        if accum is not None:
            kw["accum_out"] = accum
            wr.append(accum)
        return S.op("act", lambda g: g.activation(out=out, in_=in_, func=func, **kw), reads=rd, writes=wr)

    def copy(eng, out, in_):
        if eng == "act":
            return act(out, in_, AF.Copy)
        return S.op(eng, lambda g: g.tensor_copy(out=out, in_=in_), reads=[in_], writes=[out])

    def tt(out, in0, in1, op, eng="dve"):
        return S.op(eng, lambda g: g.tensor_tensor(out=out, in0=in0, in1=in1, op=op), reads=[in0, in1], writes=[out])

    def ts(out, in0, s1, s2, op0, op1=None, eng="dve"):
        rd = [in0] + [s for s in (s1, s2) if s is not None and not isinstance(s, (int, float))]
        if op1 is None:
            return S.op(eng, lambda g: g.tensor_scalar(out=out, in0=in0, scalar1=s1, scalar2=None, op0=op0),
                        reads=rd, writes=[out])
        return S.op(eng, lambda g: g.tensor_scalar(out=out, in0=in0, scalar1=s1, scalar2=s2, op0=op0, op1=op1),
                    reads=rd, writes=[out])

    def stt(out, in0, scalar, in1, op0, op1):
        rd = [in0, in1] + ([] if isinstance(scalar, (int, float)) else [scalar])
        return S.op("dve", lambda g: g.scalar_tensor_tensor(out=out, in0=in0, scalar=scalar, in1=in1, op0=op0, op1=op1),
                    reads=rd, writes=[out])

    def memset(ap, v, eng="dve"):
        return S.op(eng, lambda g: g.memset(ap, v), writes=[ap])

    csem = [nc.alloc_semaphore("cld"), 0]
    S.dma("sp", cst[:], cst_d[:, :], csem, sb_writes=[cst[:]])
    csem2 = [nc.alloc_semaphore("cld2"), 0]
    S.dma("sp", lnv[0:17, :], wdx_d[:, :], csem2, sb_writes=[lnv[0:17, :]])
    copy("dve", identb[:], identf)
    memset(onesb[:], 1.0)
    copy("dve", wdec[0:17, :], lnv[0:17, :])
    nsink = st[:, 128:144]
    ts(nsink, sinks, -1.0, None, ALU.mult)
    memset(gzx, 1.0)

    NCV = 7
    cv = [[nc.alloc_semaphore("cv%d" % i), 0] for i in range(NCV)]
    cvn = [0]

    def conv(dst, src):
        i = cvn[0]
        cvn[0] += 1
        return S.dma("pool", dst, src, cv[i % NCV])

    gzt = conv(wgz_d.rearrange("p (kc c) -> p kc c", kc=16),
               w_d["w_in"].rearrange("(kc p) c -> p kc c", p=128)[:, :, 3072:3088])
    gsem = [nc.alloc_semaphore("gzl"), 0]
    S.dma("sp", wgz[:], wgz_d.rearrange("p (kc c) -> p kc c", kc=16), gsem, sb_writes=[wgz[:]], extra=[gzt])

    slab_ready = []
    for s, (name, kc, ncols, pcs) in enumerate(plan):
        toks = []
        dv = wsc[s][:, 0:kc * ncols].rearrange("p (kc c) -> p kc c", kc=kc)
        for (wn, r0, c0, n, d0) in pcs:
            src = w_d[wn].rearrange("(kc p) c -> p kc c", p=128)[:, r0:r0 + kc, c0:c0 + n]
            toks.append(conv(dv[:, :, d0:d0 + n], src))
        slab_ready.append(toks)

    rsem = [[nc.alloc_semaphore("ring%d" % i), 0] for i in range(RING)]
    ld = {"n": 0}
    total_slabs = NT * NS
    slab_views = {}

    def issue_slab():
        i = ld["n"]
        if i >= total_slabs:
            return
        ld["n"] += 1
        s = i % NS
        slot = i % RING
        extra = slab_ready[s] if i < NS else ()
        n_el = plan[s][1] * plan[s][2]
        S.dma("sp", ring[:, slot, 0:n_el], wsc[s][:, 0:n_el], rsem[slot], sb_writes=[ring[:, slot, 0:n_el]], extra=extra)

    use = {"n": 0}

    def next_slab(expect, issue=True):
        i = use["n"]
        use["n"] += 1
        s = i % NS
        name, kc, ncols, _ = plan[s]
        assert name == expect, (name, expect)
        if issue:
            issue_slab()
        return ring[:, i % RING, 0:kc * ncols].rearrange("p (kc c) -> p kc c", kc=kc)

    for _ in range(RING - 1):
        issue_slab()

    xsem = [[nc.alloc_semaphore("xs%d" % i), 0] for i in range(8)]
    psem = [nc.alloc_semaphore("pld"), 0]
    osem = [[nc.alloc_semaphore("ost%d" % i), 0] for i in range(2)]
    dbg_toks = []

    def dump(name, ap, dt):
        if not debug:
            return
        shp = list(ap.shape)
        d = nc.dram_tensor("dbg_" + name, shp, dt, kind="ExternalOutput").ap()
        sem = [nc.alloc_semaphore("dbg_" + name), 0]
        dbg_toks.append(S.dma("sp", d, ap, sem, sb_reads=[ap]))

    def load_x(t, m, hf):
        r0 = t * T + m * 128
        dst = xs[:, m, hf * 1024:(hf + 1) * 1024]
        S.dma("sp", dst, x_d[r0:r0 + 128, hf * 1024:(hf + 1) * 1024], xsem[m * 2 + hf], sb_writes=[dst])

    def norm(gcol, writer):
        sq_flush()
        act(lnv, ssq_bank[:, :], AF.Ln, bias=EPS, scale=1.0 / D)
        rb = bank()
        act(rb[:, :], lnv, AF.Exp, scale=-0.5)
        for k in range(16):
            writer(k, rb[:, :], gains[:, gcol + k:gcol + k + 1])

    def norm_to_uT(gcol):
        norm(gcol, lambda k, r, g: stt(uT[:, k, :], hT[:, k, :], g, r, ALU.mult, ALU.mult))

    ev = {"n": 0}

    def evac_eng():
        ev["n"] += 1
        return "act" if ev["n"] % 2 == 0 else "dve"

    def dense_b_pair(sl0, sl1, kc, rhs_of_k):
        pbs = [bank() for _ in range(4)]
        for k in range(kc):
            for c4 in range(4):
                sl = sl0 if c4 < 2 else sl1
                mm(pbs[c4][:, :], lhsT=sl[:, k, 128 * (c4 % 2):128 * (c4 % 2) + 128], rhs=rhs_of_k(k),
                   start=(k == 0), stop=(k == kc - 1), signal=(k == kc - 1))
        return pbs

    def dense_b(slab, kc, c0, rhs_of_k):
        pb = bank()
        for k in range(kc):
            mm(pb[:, :], lhsT=slab[:, k, c0:c0 + 128], rhs=rhs_of_k(k), start=(k == 0), stop=(k == kc - 1),
               signal=(k == kc - 1))
        return pb

    def s1_gen(t):
        first = (t % TPS == 0)
        S.stage = "S1"
        if t == 0:
            for m_ in range(4):
                for hf_ in range(2):
                    load_x(0, m_, hf_)
        S.dma("sp", pst, p_d[t * T:(t + 1) * T, :].rearrange("(m p) c -> p m c", p=128), psem, sb_writes=[pst])
        for m in range(4):
            for kg in range(4):
                S.stage = "S1"
                pb = bank()
                for j in range(4):
                    k = 4 * kg + j
                    tr(pb[:, j * 128:(j + 1) * 128], xs[:, m, k * 128:(k + 1) * 128], identf, signal=(j == 3))
                copy(evac_eng(), hT[:, 4 * kg:4 * kg + 4, m * 128:(m + 1) * 128],
                     pb[:, :].rearrange("p (a b) -> p a b", a=4))
                if m == 3:
                    for k in range(4 * kg, 4 * kg + 4):
                        sq_accum(k)
                yield
            S.stage = "S1"
            pb = bank()
            for j in range(2):
                tr(pb[:, j * 128:(j + 1) * 128], pst[:, m, j * 128:(j + 1) * 128], identf, signal=(j == 1))
            copy(evac_eng(), pT[:, :, m * 128:(m + 1) * 128], pb[:, 0:256].rearrange("p (a b) -> p a b", a=2))
        if first:
            memset(Sst[:], 0.0)
            memset(Sbf[:], 0.0)
        else:
            copy("dve", skx[:, :, 0:128], skx[:, :, 512:640])
            copy("dve", svx[:, 0, :], svx[:, 4, :])

    def s10_gen(t):
        for m in range(4):
            for kg in range(4):
                S.stage = "S10"
                pb = bank()
                for j in range(4):
                    k = 4 * kg + j
                    tr(pb[:, j * 128:(j + 1) * 128], oT[:, k, m * 128:(m + 1) * 128], identf, signal=(j == 3))
                copy(evac_eng(), osb[:, kg * 512:(kg + 1) * 512], pb[:, :])
                if kg % 2 == 1:
                    hf = kg // 2
                    r0 = t * T + m * 128
                    S.dma("sp", out_d[r0:r0 + 128, hf * 1024:(hf + 1) * 1024], osb[:, hf * 1024:(hf + 1) * 1024],
                          osem[hf], sb_reads=[osb[:, hf * 1024:(hf + 1) * 1024]])
                yield

    for _ in s1_gen(0):
        pass
    for t in range(NT):
        first = (t % TPS == 0)
        norm_to_uT(0)

        S.stage = "S2"
        sl0 = next_slab("gq0")
        sl1 = next_slab("gq1", issue=False)
        pbs = dense_b_pair(sl0, sl1, 16, lambda k: uT[:, k, :])
        issue_slab()
        for c4 in range(4):
            copy(evac_eng(), qT[:, c4, :], pbs[c4][:, :])
        for j in range(2):
            sl = next_slab("gk%d" % j)
            for c in range(2):
                pb = dense_b(sl, 16, 128 * c, lambda k: uT[:, k, :])
                copy(evac_eng(), kT[:, 2 * j + c, :], pb[:, :])

        def dense_a(sl, m, half, pb):
            for k in range(16):
                mm(pb[:, half * 256:(half + 1) * 256], lhsT=uT[:, k, m * 128:(m + 1) * 128], rhs=sl[:, k, :],
                   start=(k == 0), stop=(k == 15), signal=(k == 15))

        for j in range(4):
            sl = next_slab("gv%d" % j)
            for mp in range(2):
                pb = bank()
                for hf in range(2):
                    dense_a(sl, 2 * mp + hf, hf, pb)
                copy(evac_eng(), vt[:, 2 * mp:2 * mp + 2, 256 * j:256 * (j + 1)],
                     pb[:, :].rearrange("p (a b) -> p a b", a=2))
        for j in range(4):
            sl = next_slab("gr%d" % j)
            for mp in range(2):
                pb = bank()
                for hf in range(2):
                    dense_a(sl, 2 * mp + hf, hf, pb)
                act(fA[:, 0, :], pb[:, :], AF.Tanh, scale=0.5)
                stt(fA[:, 1, :], fA[:, 0, :], 1.0, pb[:, :], ALU.add, ALU.mult)
                gsl = gnb[:, (256 * j) % 256:(256 * j) % 256 + 256]
                for hf in range(2):
                    tt(G2[:, 2 * mp + hf, 256 * j:256 * (j + 1)], fA[:, 1, hf * 256:(hf + 1) * 256], gsl, ALU.mult)
        pb = bank()
        for k in range(16):
            mm(pb[0:16, :], lhsT=wgz[:, k, :], rhs=uT[:, k, :], start=(k == 0), stop=(k == 15), signal=(k == 15))
        copy("dve", gzx[0:16, :], pb[0:16, :])
        S.stage = "S3"
        def dense_swa_gen():
            for i in range(4):
                sl = next_slab("sq%d" % i)
                for c in range(2):
                    pb = dense_b(sl, 16, 128 * c, lambda k: uT[:, k, :])
                    copy(evac_eng(), sqT[:, 2 * i + c, :], pb[:, :])
                    yield
            sl = next_slab("sk")
            for c in range(2):
                pb = dense_b(sl, 16, 128 * c, lambda k: uT[:, k, :])
                copy(evac_eng(), skx[:, c, 128:640], pb[:, :])
                yield
            sl = next_slab("sv")
            for mp in range(2):
                pb = bank()
                for hf in range(2):
                    dense_a(sl, 2 * mp + hf, hf, pb)
                copy(evac_eng(), svx[:, 1 + 2 * mp:3 + 2 * mp, :], pb[:, :].rearrange("p (a b) -> p a b", a=2))
                yield

        def gla_gen():
            for m in range(4):
                tk = slice(m * 128, (m + 1) * 128)
                pz = bank()
                mm(pz[:, :], lhsT=gzx[0:17, tk], rhs=wdec[0:17, :])
                act(gE2, pz[:, :], AF.Exp, scale=-1.0)
                act(gL, gE2, AF.Ln, bias=1.0)
                yield
                pbt = bank()
                for h in range(4):
                    mm(pbt[:, h * 128:(h + 1) * 128], lhsT=gL[:, h * 128:(h + 1) * 128], rhs=trif, signal=(h == 3))
                act(gE1, pbt[:, :], AF.Exp, scale=-1.0 / 16.0)
                act(gE2, pbt[:, :], AF.Exp, scale=1.0 / 16.0)
                yield
                stt(qtl, qT[:, :, tk], 128.0 ** -0.5, gE1.rearrange("p (a b) -> p a b", a=4), ALU.mult, ALU.mult)
                tt(ktl, kT[:, :, tk], gE2.rearrange("p (a b) -> p a b", a=4), ALU.mult)
                yield
                pk = bank()
                pkb = pk[:, 0:256].bitcast(BF16)
                for h in range(4):
                    tr(pkb[:, h * 128:(h + 1) * 128], ktl[:, h, :], identb[:], signal=(h == 3))
                copy("act", ktm, pkb)
                pa = bank()
                for h in range(4):
                    mm(pa[:, h * 128:(h + 1) * 128], lhsT=ktl[:, h, :], rhs=qtl[:, h, :], signal=(h == 3))
                tt(gAT, pa[:, :].rearrange("p (a b) -> p a b", a=4), tri4.rearrange("p (a b) -> p a b", a=4), ALU.mult)
                yield
                po = [bank(), bank()]
                for h in range(4):
                    o = po[h // 2][:, (h % 2) * 256:(h % 2 + 1) * 256]
                    mm(o, lhsT=gAT[:, h, :], rhs=vt[:, m, h * 256:(h + 1) * 256], start=True, stop=False, signal=False)
                    mm(o, lhsT=qtl[:, h, :], rhs=Sbf[:, h, :], start=False, stop=True, signal=(h % 2 == 1))
                pd = [bank(), bank()]
                for h in range(4):
                    mm(pd[h // 2][:, (h % 2) * 256:(h % 2 + 1) * 256], lhsT=ktm[:, h * 128:(h + 1) * 128],
                       rhs=vt[:, m, h * 256:(h + 1) * 256], signal=(h % 2 == 1))
                for h in range(4):
                    o = po[h // 2][:, (h % 2) * 256:(h % 2 + 1) * 256]
                    act(fA[:, 2, 0:256], o, AF.Square, accum=st[:, h:h + 1])
                act(st[:, 4:8], st[:, 0:4], AF.Ln, bias=EPS, scale=1.0 / 256.0)
                act(st[:, 8:12], st[:, 4:8], AF.Exp, scale=-0.5, bias=math.log(0.5))
                for h in range(4):
                    o = po[h // 2][:, (h % 2) * 256:(h % 2 + 1) * 256]
                    stt(yat[:, h * 256:(h + 1) * 256], o, st[:, 8 + h:9 + h], G2[:, m, h * 256:(h + 1) * 256],
                        ALU.mult, ALU.mult)
                yield
                for i2 in range(2):
                    tt(Sst[:, 2 * i2:2 * i2 + 2, :], Sst[:, 2 * i2:2 * i2 + 2, :],
                       pd[i2][:, :].rearrange("p (a b) -> p a b", a=2), ALU.add)
                dl = gE1[:, 127:128]
                dec = bass.AP(dl.tensor, dl.offset, [list(dl.ap[0]), [128, 4], [0, 256]])
                tt(Sst[:], Sst[:], dec, ALU.mult)
                copy("act", Sbf[:], Sst[:])
                yield
                pt_ = bank()
                ptb = pt_[:, :].bitcast(BF16)
                for c in range(8):
                    tr(ptb[:, c * 128:(c + 1) * 128], yat[:, c * 128:(c + 1) * 128], identb[:], signal=(c == 7))
                copy("act", yaT[:, :, tk], ptb.rearrange("p (a b) -> p a b", a=8))
                yield

        gg = gla_gen()
        gla_live = [True]

        def gla_steps(n):
            for _ in range(n):
                if gla_live[0]:
                    try:
                        next(gg)
                    except StopIteration:
                        gla_live[0] = False

        for _ in dense_swa_gen():
            gla_steps(2)

        def swa_P1(u):
            m, kvp = u // 2, u % 2
            tk = slice(m * 128, (m + 1) * 128)
            k0 = 128 if (first and m == 0) else 0
            b = u % 2
            so = 16 + 56 * kvp
            ps_ = [bank() for _ in range(4)]
            for jj in (0, 4, 1, 5, 2, 6, 3, 7):
                hf, j = jj // 4, jj % 4
                prow = slice(hf * 64, (hf + 1) * 64)
                c = kvp * 4 + j
                mm(ps_[jj // 2][:, (jj % 2) * 256 + k0:(jj % 2 + 1) * 256], lhsT=sqT[prow, c, tk],
                   rhs=skx[prow, kvp, m * 128 + k0:m * 128 + 256], signal=True)
            for jj in range(8):
                h = 8 * kvp + jj
                stt(zz[b][:, jj, k0:256], dist8[:, k0:256], -SLOPES[h],
                    ps_[jj // 2][:, (jj % 2) * 256 + k0:(jj % 2 + 1) * 256], ALU.mult, ALU.add)
            S.op("dve", lambda g, o=st[:, so:so + 8], i=zz[b][:, :, k0:256]: g.tensor_reduce(out=o, in_=i, axis=AX.X, op=ALU.max),
                 reads=[zz[b][:, :, k0:256]], writes=[st[:, so:so + 8]])

        def swa_P1b(u):
            m, kvp = u // 2, u % 2
            k0 = 128 if (first and m == 0) else 0
            b = u % 2
            so = 16 + 56 * kvp
            stt(st[:, so + 8:so + 16], st[:, so:so + 8], -0.125, nsink[:, 8 * kvp:8 * kvp + 8], ALU.mult, ALU.min)
            tt(st[:, so + 16:so + 24], sinks[:, 8 * kvp:8 * kvp + 8], st[:, so + 8:so + 16], ALU.add)
            act(st[:, so + 24:so + 32], st[:, so + 16:so + 24], AF.Exp)
            for jj in range(8):
                act(Pb[b][:, jj, k0:256], zz[b][:, jj, k0:256], AF.Exp, bias=st[:, so + 8 + jj:so + 9 + jj], scale=0.125,
                    accum=st[:, so + 32 + jj:so + 33 + jj])

        def swa_P2(u):
            m, kvp = u // 2, u % 2
            k0 = 128 if (first and m == 0) else 0
            b = u % 2
            so = 16 + 56 * kvp
            tt(st[:, so + 40:so + 48], st[:, so + 24:so + 32], st[:, so + 32:so + 40], ALU.add)
            S.op("dve", lambda g, o=st[:, so + 48:so + 56], i=st[:, so + 40:so + 48]: g.reciprocal(out=o, in_=i),
                 reads=[st[:, so + 40:so + 48]], writes=[st[:, so + 48:so + 56]])
            halves = [1] if k0 else [0, 1]
            n_tr = 4 * len(halves)
            for i2 in range(2):
                pp_ = bank()
                ppb = pp_[:, :].bitcast(BF16)
                cnt = 0
                for j in range(4):
                    jj = 4 * i2 + j
                    for hh in halves:
                        cnt += 1
                        tr(ppb[:, (2 * j + hh) * 128:(2 * j + hh + 1) * 128], Pb[b][:, jj, hh * 128:(hh + 1) * 128],
                           identb[:], signal=(cnt == n_tr))
                dst = PTb[b][:, 8 * i2:8 * i2 + 8, :]
                eng = "act"
                if k0:
                    copy(eng, dst.rearrange("p (j h) q -> p j h q", h=2)[:, :, 1, :],
                         ppb.rearrange("p (j h q) -> p j h q", j=4, h=2)[:, :, 1, :])
                else:
                    copy(eng, dst, ppb.rearrange("p (a b) -> p a b", a=8))

        def swa_P3(u):
            m, kvp = u // 2, u % 2
            tk = slice(m * 128, (m + 1) * 128)
            k0 = 128 if (first and m == 0) else 0
            b = u % 2
            so = 16 + 56 * kvp
            halves = [1] if k0 else [0, 1]
            py = bank()
            for jj in range(8):
                kv = 2 * kvp + jj // 4
                o = py[:, jj * 64:(jj + 1) * 64]
                for ii, hh in enumerate(halves):
                    mm(o, lhsT=PTb[b][:, 2 * jj + hh, :], rhs=svx[:, m + hh, kv * 64:(kv + 1) * 64],
                       start=(ii == 0), stop=(ii == len(halves) - 1),
                       signal=(ii == len(halves) - 1 and jj == 7))
            yb_ = ybn[m % 2]
            tt(yb_[:, 8 * kvp * 64:(8 * kvp + 8) * 64].rearrange("p (a b) -> p a b", a=8),
               py[:, :].rearrange("p (a b) -> p a b", a=8),
               st[:, so + 48:so + 56].unsqueeze(2).broadcast_to([128, 8, 64]), ALU.mult)
            if kvp == 1:
                pt_ = bank()
                ptb = pt_[:, :].bitcast(BF16)
                for c in range(8):
                    tr(ptb[:, c * 128:(c + 1) * 128], yb_[:, c * 128:(c + 1) * 128], identb[:], signal=(c == 7))
                copy("dve", ybT[:, :, tk], ptb.rearrange("p (a b) -> p a b", a=8))

        for s_ in range(8 + 3):
            if s_ < 8:
                swa_P1(s_)
            if 0 <= s_ - 2 < 8:
                swa_P2(s_ - 2)
            if 0 <= s_ - 3 < 8:
                swa_P3(s_ - 3)
            if s_ < 8:
                swa_P1b(s_)
            gla_steps(2)
        gla_steps(100)

        if t == 0:
            dump("uT", uT[:], BF16)
            dump("yaT", yaT, BF16)
            dump("ybT", ybT, BF16)
            dump("qT", qT, BF16)
            dump("vt", vt, BF16)
            dump("G2", G2, BF16)
            dump("sqT", sqT, BF16)
            dump("skx", skx, BF16)
            dump("svx", svx, BF16)
        S.stage = "S4"
        for j in range(8):
            slA = next_slab("A%d" % j)
            pA = [dense_b(slA, 8, 128 * c, lambda k: yaT[:, k, :]) for c in range(2)]
            sga = next_slab("ga%d" % j)
            for c in range(2):
                pg = dense_b(sga, 16, 128 * c, lambda k: uT[:, k, :])
                act(fA[:, 0, :], pg[:, :], AF.Tanh, scale=0.5)
                stt(fA[:, 1 + c, :], fA[:, 0, :], 1.0, pA[c][:, :], ALU.add, ALU.mult)
            slB = next_slab("B%d" % j)
            pB = [dense_b(slB, 8, 128 * c, lambda k: ybT[:, k, :]) for c in range(2)]
            sgb = next_slab("gb%d" % j)
            for c in range(2):
                f = 2 * j + c
                pg = dense_b(sgb, 16, 128 * c, lambda k: uT[:, k, :])
                act(fA[:, 0, :], pg[:, :], AF.Tanh, scale=0.5)
                stt(fA[:, 0, :], fA[:, 0, :], 1.0, pB[c][:, :], ALU.add, ALU.mult)
                tt(mgT[:, f, :], fA[:, 0, :], fA[:, 1 + c, :], ALU.add)

        if t == 0:
            dump("mgT", mgT, BF16)
        S.stage = "S5"
        for j in range(8):
            sl = next_slab("wo%d" % j)
            for c in range(2):
                f = 2 * j + c
                pb = dense_b(sl, 16, 128 * c, lambda k: mgT[:, k, :])
                stt(hT[:, f, :], pb[:, :], 0.5, hT[:, f, :], ALU.mult, ALU.add)
                sq_accum(f)

        if t == 0:
            dump("h1", hT[:], F32)
        S.stage = "S6-8"
        norm_to_uT(16)
        if t + 1 < NT:
            for m_ in range(4):
                for hf_ in range(2):
                    load_x(t + 1, m_, hf_)
        for q in range(4):
            ab = actb[q % 2]
            j_start = 0
            if q == 0:
                sl0 = next_slab("up0_0")
                sl1 = next_slab("up0_1", issue=False)
                pbs = dense_b_pair(sl0, sl1, 16, lambda k: uT[:, k, :])
                issue_slab()
                for c4 in range(4):
                    act(fA[:, c4 % 3, :], pbs[c4][:, :], AF.Square)
                    stt(ab[:, c4, :], pbs[c4][:, :], 0.0, fA[:, c4 % 3, :], ALU.is_gt, ALU.mult)
                j_start = 2
            for j in range(j_start, 8):
                sl = next_slab("up%d_%d" % (q, j))
                for c in range(2):
                    pb = dense_b(sl, 16, 128 * c, lambda k: uT[:, k, :])
                    act(fA[:, (2 * j + c) % 3, :], pb[:, :], AF.Square)
                    stt(ab[:, 2 * j + c, :], pb[:, :], 0.0, fA[:, (2 * j + c) % 3, :], ALU.is_gt, ALU.mult)
            for j in range(8):
                sl = next_slab("dn%d_%d" % (q, j))
                for c in range(2):
                    f = 2 * j + c
                    pb = dense_b(sl, 16, 128 * c, lambda k: ab[:, k, :])
                    tt(hT[:, f, :], pb[:, :], hT[:, f, :], ALU.add)
                    if q == 3:
                        sq_accum(f)

        if t == 0:
            dump("h2", hT[:], F32)
        S.stage = "S9"
        norm_to_uT(32)
        for j in range(8):
            slp = next_slab("pp%d" % j)
            ppb_ = [dense_b(slp, 2, 128 * c, lambda k: pT[:, k, :]) for c in range(2)]
            sl = next_slab("pg%d" % j)
            for c in range(2):
                f = 2 * j + c
                pp_ = ppb_[c]
                pg = dense_b(sl, 16, 128 * c, lambda k: uT[:, k, :])
                act(fA[:, c, :], pg[:, :], AF.Tanh, scale=0.5)
                stt(fA[:, c, :], fA[:, c, :], 1.0, pp_[:, :], ALU.add, ALU.mult)
                stt(hT[:, f, :], fA[:, c, :], 0.5, hT[:, f, :], ALU.mult, ALU.add)
                sq_accum(f)

        if t == 0:
            dump("h3", hT[:], F32)
        S.stage = "S10"
        norm(48, lambda k, r, g: stt(oT[:, k, :], hT[:, k, :], g, r, ALU.mult, ALU.mult))
        g10 = s10_gen(t)
        g1 = s1_gen(t + 1) if t + 1 < NT else iter(())
        live = [True, True]
        while live[0] or live[1]:
            for gi, gg_ in enumerate((g10, g1)):
                if live[gi]:
                    try:
                        next(gg_)
                    except StopIteration:
                        live[gi] = False

    S.wait_only("sp", [(o_[0], o_[1]) for o_ in osem] + dbg_toks)
    for en in ("pe", "act", "dve"):
        assert not getattr(S.eng[en], "pending", False), en

    nc_sched_holder.clear()
    nc_sched_holder.append(S)
    with nc.Block() as block:
        @block.sync
        def _(g):
            S.emit("sp", g)

        @block.gpsimd
        def _(g):
            S.emit("pool", g)

        @block.tensor
        def _(g):
            S.emit("pe", g)

        @block.scalar
        def _(g):
            S.emit("act", g)

        @block.vector
        def _(g):
            S.emit("dve", g)
    return nc


def make_consts(norm_mix, norm_mlp, norm_ple, norm_final, attn_sinks, gla_norm):
    c = np.zeros((128, C_END), np.float32)
    for n, g in enumerate((norm_mix, norm_mlp, norm_ple, norm_final)):
        c[:, C_GAIN + 16 * n:C_GAIN + 16 * (n + 1)] = np.asarray(g, np.float32).reshape(16, 128).T
    c[:, C_SINK:C_SINK + 16] = np.asarray(attn_sinks, np.float32).reshape(1, 16)
    c[:, C_GN:C_GN + 256] = np.asarray(gla_norm, np.float32).reshape(1, 256)
    c[:, C_ID:C_ID + 128] = np.eye(128, dtype=np.float32)
    tri = (np.arange(128)[:, None] <= np.arange(128)[None, :]).astype(np.float32)
    c[:, C_TRI:C_TRI + 512] = np.tile(tri, (1, 4))
    i = np.arange(128)[:, None]
    j = np.arange(256)[None, :]
    dist = i + 128 - j
    ok = (dist >= 0) & (dist < 128)
    c[:, C_DIST:C_DIST + 256] = np.where(ok, 8.0 * dist, 1e9).astype(np.float32)
    return c


_NC_CACHE = {}


def kernel(x, p, norm_mix, w_in, w_decay, b_decay, gla_norm, attn_sinks, w_branch_a, w_branch_b,
           w_out, norm_mlp, w_up, w_down, norm_ple, w_ple_gate, w_ple_proj, norm_final):
    x = np.asarray(x, np.float32)
    B, SEQ, _ = x.shape
    n_cores = N_CORES
    per = B // n_cores
    key = (per, SEQ)
    if key not in _NC_CACHE:
        _NC_CACHE[key] = build(n_seq=per, seq_len=SEQ)
    nc = _NC_CACHE[key]
    cst = make_consts(np.asarray(norm_mix)[0], np.asarray(norm_mlp)[0], np.asarray(norm_ple)[0],
                      np.asarray(norm_final), np.asarray(attn_sinks)[0], np.asarray(gla_norm)[0])
    wdx = np.concatenate([np.asarray(w_decay, np.float32)[0], np.asarray(b_decay, np.float32)[0][None, :]], axis=0)
    wts = {
        "w_in": np.ascontiguousarray(np.asarray(w_in, np.float32)[0]),
        "w_branch_a": np.ascontiguousarray(np.asarray(w_branch_a, np.float32)[0]),
        "w_branch_b": np.ascontiguousarray(np.asarray(w_branch_b, np.float32)[0]),
        "w_out": np.ascontiguousarray(np.asarray(w_out, np.float32)[0]),
        "w_up": np.ascontiguousarray(np.asarray(w_up, np.float32)[0]),
        "w_down": np.ascontiguousarray(np.asarray(w_down, np.float32)[0]),
        "w_ple_gate": np.ascontiguousarray(np.asarray(w_ple_gate, np.float32)[0]),
        "w_ple_proj": np.ascontiguousarray(np.asarray(w_ple_proj, np.float32)[0]),
    }
    pp = np.asarray(p, np.float32)[0]
    in_maps = []
    for c in range(n_cores):
        m = {"x": np.ascontiguousarray(x[c * per:(c + 1) * per].reshape(per * SEQ, D)),
             "p": np.ascontiguousarray(pp[c * per:(c + 1) * per].reshape(per * SEQ, 256)),
             "cst": cst, "wdx": wdx}
        m.update(wts)
        in_maps.append(m)
    res = run_bass_kernel_spmd(nc, in_maps, core_ids=list(range(n_cores)))
    out = np.concatenate([np.asarray(r["out"]).reshape(per, SEQ, D) for r in res.results], axis=0)
    return out.astype(np.float32)
```

```python
import math
import numpy as np
import concourse.bass as bass
import concourse.mybir as mybir
from concourse.bass_utils import run_bass_kernel_spmd

F32 = mybir.dt.float32
BF16 = mybir.dt.bfloat16
AF = mybir.ActivationFunctionType
ALU = mybir.AluOpType
AX = mybir.AxisListType

D = 2048
T = 512
EPS = 1e-6
N_CORES = 8
SLOPES = [2.0 ** (-8.0 * (h + 1) / 16.0) for h in range(16)]

C_GAIN, C_SINK, C_GN, C_ID, C_TRI, C_DIST, C_END = 0, 64, 80, 336, 464, 976, 1232

A_QT, A_KT, A_V, A_G2, A_SQT, A_YAT, A_YBT = 0, 4096, 8192, 16384, 24576, 32768, 40960
A_SKX, A_SVX, A_GZX = 49152, 51712, 54272
A_GS = 55296
A_SS = 67584
A_END = 67584 + 2 * 16384 + 2 * 2048
RING = 3
SAME_ENG_WAR = False


def _esz(dt):
    return 2 if dt == BF16 else 4


def _iv(ap):
    pat = ap.ap
    row = pat[0][0]
    off = ap.offset
    lo = off % row if row > 0 else off
    ext = 1
    for s, c in pat[1:]:
        ext += (c - 1) * abs(s)
    e = _esz(ap.dtype)
    name = ap.tensor.name
    if name.startswith("bank"):
        return (name, 0, 2048)
    return (name, lo * e, (lo + ext) * e)


class _Eng:
    def __init__(self, name, sem):
        self.name = name
        self.sem = sem
        self.count = 0
        self.waited = {}
        self.ops = []


class Sched:
    def __init__(self, nc):
        self.nc = nc
        self.eng = {}
        for n in ("pe", "act", "dve", "pool", "sp"):
            self.eng[n] = _Eng(n, nc.alloc_semaphore("prog_" + n))
        self.segs = {}
        self.semobj = {}
        self.stage = ""

    def _split(self, key, x):
        L = self.segs.setdefault(key, [])
        for i, s in enumerate(L):
            if s[0] < x < s[1]:
                L.insert(i + 1, [x, s[1], s[2], dict(s[3])])
                s[1] = x
                return

    def _cover(self, key, lo, hi):
        self._split(key, lo)
        self._split(key, hi)
        return [s for s in self.segs.setdefault(key, []) if s[0] >= lo and s[1] <= hi]

    def _deps(self, reads, writes):
        toks = []
        for (key, lo, hi) in reads:
            for s in self._cover(key, lo, hi):
                if s[2] is not None:
                    toks.append((s[2], "w"))
                if key.startswith("bank"):
                    for sem, v in s[3].items():
                        toks.append(((sem, v), "r"))
        for (key, lo, hi) in writes:
            for s in self._cover(key, lo, hi):
                if s[2] is not None:
                    toks.append((s[2], "w"))
                for sem, v in s[3].items():
                    toks.append(((sem, v), "r"))
        return toks

    def _commit(self, tok, reads, writes):
        for (key, lo, hi) in reads:
            for s in self._cover(key, lo, hi):
                if s[3].get(tok[0], 0) < tok[1]:
                    s[3][tok[0]] = tok[1]
            self._fill(key, lo, hi, None, tok)
        for (key, lo, hi) in writes:
            L = self.segs.setdefault(key, [])
            self._split(key, lo)
            self._split(key, hi)
            L[:] = [s for s in L if not (s[0] >= lo and s[1] <= hi)]
            L.append([lo, hi, tok, {}])
            L.sort(key=lambda s: s[0])

    def _fill(self, key, lo, hi, w, rtok):
        L = self.segs.setdefault(key, [])
        cur = lo
        new = []
        for s in sorted(L, key=lambda s: s[0]):
            if s[1] <= lo or s[0] >= hi:
                continue
            if s[0] > cur:
                new.append([cur, s[0], w, {rtok[0]: rtok[1]}])
            cur = max(cur, s[1])
        if cur < hi:
            new.append([cur, hi, w, {rtok[0]: rtok[1]}])
        if new:
            L.extend(new)
            L.sort(key=lambda s: s[0])

    def _waits(self, e, toks):
        need = {}
        for (sem, v), kind in toks:
            if sem is e.sem:
                if e.name in ("pe", "sp") or (kind == "r" and not SAME_ENG_WAR):
                    continue
            if v > need.get(sem, 0):
                need[sem] = v
        out = []
        for sem, v in need.items():
            if e.waited.get(sem, 0) < v:
                e.waited[sem] = v
                out.append((sem, v))
        return out

    def op(self, en, fn, reads=(), writes=(), signal=True, extra=()):
        e = self.eng[en]
        r = [_iv(a) for a in reads]
        w = [_iv(a) for a in writes]
        toks = self._deps(r, w) + [(t, "w") for t in extra]
        waits = self._waits(e, toks)
        tok = (e.sem, e.count + 1)
        if signal:
            e.count += 1
        e.pending = not signal
        e.ops.append((waits, fn, (e.sem, 1) if signal else None, self.stage))
        self._commit(tok, r, w)
        return tok

    def dma(self, en, out, in_, dsem, sb_reads=(), sb_writes=(), extra=()):
        e = self.eng[en]
        r = [_iv(a) for a in sb_reads]
        w = [_iv(a) for a in sb_writes]
        toks = self._deps(r, w) + [(t, "w") for t in extra]
        if dsem[1] > 0:
            toks.append(((dsem[0], dsem[1]), "w"))
        waits = self._waits(e, toks)
        dsem[1] += 16
        tok = (dsem[0], dsem[1])
        e.ops.append((waits, (lambda g, o=out, i=in_: g.dma_start(out=o, in_=i)), (dsem[0], 16), self.stage))
        self._commit(tok, r, w)
        return tok

    def wait_only(self, en, toks):
        e = self.eng[en]
        waits = self._waits(e, [(t, "w") for t in toks])
        if waits:
            e.ops.append((waits, None, None, self.stage))

    def emit(self, en, g):
        for waits, fn, inc, _lab in self.eng[en].ops:
            for sem, v in waits:
                g.wait_ge(sem, v)
            if fn is None:
                continue
            ins = fn(g)
            if inc is not None:
                ins.then_inc(inc[0], inc[1])


def _slab_plan():
    P = []

    def simple(name, w, c0, kc=16, ncols=256, r0=0):
        P.append((name, kc, ncols, [(w, r0, c0, ncols, 0)]))

    for j in range(2):
        simple("gq%d" % j, "w_in", 256 * j)
    for j in range(2):
        simple("gk%d" % j, "w_in", 512 + 256 * j)
    for j in range(4):
        simple("gv%d" % j, "w_in", 1024 + 256 * j)
    for j in range(4):
        simple("gr%d" % j, "w_in", 2048 + 256 * j)
    for i in range(4):
        pcs = []
        for cc in range(2):
            c = 2 * i + cc
            ha = c if c < 4 else 8 + (c - 4)
            hb = ha + 4
            pcs.append(("w_in", 0, 3088 + 64 * ha, 64, 128 * cc))
            pcs.append(("w_in", 0, 3088 + 64 * hb, 64, 128 * cc + 64))
        P.append(("sq%d" % i, 16, 256, pcs))
    simple("sk", "w_in", 4112)
    simple("sv", "w_in", 4368)
    for j in range(8):
        simple("A%d" % j, "w_branch_a", 256 * j, kc=8, ncols=256)
        simple("ga%d" % j, "w_in", 4624 + 256 * j)
        simple("B%d" % j, "w_branch_b", 256 * j, kc=8, ncols=256)
        simple("gb%d" % j, "w_in", 6672 + 256 * j)
    for j in range(8):
        simple("wo%d" % j, "w_out", 256 * j)
    for q in range(4):
        for j in range(8):
            simple("up%d_%d" % (q, j), "w_up", 2048 * q + 256 * j)
        for j in range(8):
            simple("dn%d_%d" % (q, j), "w_down", 256 * j, r0=16 * q)
    for j in range(8):
        simple("pp%d" % j, "w_ple_proj", 256 * j, kc=2, ncols=256)
        simple("pg%d" % j, "w_ple_gate", 256 * j)
    return P


W_SHAPES = {
    "w_in": (2048, 8720), "w_branch_a": (1024, 2048), "w_branch_b": (1024, 2048),
    "w_out": (2048, 2048), "w_up": (2048, 8192), "w_down": (8192, 2048),
    "w_ple_gate": (2048, 2048), "w_ple_proj": (256, 2048),
}


nc_sched_holder = []


def build(n_seq=2, seq_len=2048, debug=None):
    NTOK = n_seq * seq_len
    NT = NTOK // T
    TPS = seq_len // T
    nc = bass.Bass("TRN2", target_bir_lowering=False)
    x_d = nc.dram_tensor("x", [NTOK, D], F32, kind="ExternalInput").ap()
    p_d = nc.dram_tensor("p", [NTOK, 256], F32, kind="ExternalInput").ap()
    cst_d = nc.dram_tensor("cst", [128, C_END], F32, kind="ExternalInput").ap()
    wdx_d = nc.dram_tensor("wdx", [17, 512], F32, kind="ExternalInput").ap()
    w_d = {n: nc.dram_tensor(n, list(s), F32, kind="ExternalInput").ap() for n, s in W_SHAPES.items()}
    out_d = nc.dram_tensor("out", [NTOK, D], F32, kind="ExternalOutput").ap()
    plan = _slab_plan()
    NS = len(plan)
    wsc = nc.dram_tensor("wsc", [NS, 128, 4096], BF16).ap()
    wgz_d = nc.dram_tensor("wgzsc", [128, 256], BF16).ap()

    S = Sched(nc)
    A = nc.alloc_sbuf_tensor
    hT = A("hT", [128, 16, T], F32)
    uT = A("uT", [128, 16, T], BF16)
    ring = A("ring", [128, RING, 4096], BF16)
    cst = A("cst_sb", [128, C_END], F32)
    identb = A("identb", [128, 128], BF16)
    onesb = A("onesb", [128, 128], BF16)
    wdec = A("wdec", [32, 512], BF16)
    wgz = A("wgz", [128, 16, 16], BF16)
    Sst = A("Sst", [128, 4, 256], F32)
    Sbf = A("Sbf", [128, 4, 256], BF16)
    pT = A("pT", [128, 2, T], BF16)
    sqs = A("sqs", [128, 4, T], BF16)
    fA = A("fA", [128, 3, T], F32)
    st = A("stats", [128, 144], F32)
    lnv = fA[:, 2, :]
    arena = A("arena", [128, A_END // 2], BF16)
    banks = [nc.alloc_psum_tensor("bank%d" % i, [128, 512], F32) for i in range(8)]

    def av(off, nbytes, dt, pat=None, parts=128, **kw):
        a = arena[0:parts, off // 2:(off + nbytes) // 2]
        if dt != BF16:
            a = a.bitcast(dt)
        if pat:
            a = a.rearrange(pat, **kw)
        return a

    qT = av(A_QT, 4096, BF16, "p (a b) -> p a b", a=4)
    kT = av(A_KT, 4096, BF16, "p (a b) -> p a b", a=4)
    vt = av(A_V, 8192, BF16, "p (a b) -> p a b", a=4)
    mgT = av(A_QT, 16384, BF16, "p (a b) -> p a b", a=16)
    G2 = av(A_G2, 8192, BF16, "p (a b) -> p a b", a=4)
    sqT = av(A_SQT, 8192, BF16, "p (a b) -> p a b", a=8)
    yaT = av(A_YAT, 8192, BF16, "p (a b) -> p a b", a=8)
    ybT = av(A_YBT, 8192, BF16, "p (a b) -> p a b", a=8)
    skx = av(A_SKX, 2560, BF16, "p (a b) -> p a b", a=2)
    svx = av(A_SVX, 2560, BF16, "p (a b) -> p a b", a=5)
    gzx = av(A_GZX, 1024, BF16, parts=32)
    gL = av(A_GS, 2048, F32)
    gE1 = av(A_GS + 2048, 2048, F32)
    gE2 = av(A_GS + 4096, 2048, F32)
    qtl = av(A_GS + 6144, 1024, BF16, "p (a b) -> p a b", a=4)
    ktl = av(A_GS + 7168, 1024, BF16, "p (a b) -> p a b", a=4)
    ktm = av(A_GS + 8192, 1024, BF16)
    gAT = av(A_GS + 9216, 1024, BF16, "p (a b) -> p a b", a=4)
    yat = av(A_GS + 10240, 2048, BF16)
    xs = av(A_GS, 32768, F32, "p (a b) -> p a b", a=4)
    pst = av(A_GS + 32768, 4096, F32, "p (a b) -> p a b", a=4)
    osb = av(A_GS + 36864, 8192, F32)
    zz = [av(A_SS + 16384 * i, 8192, F32, "p (a b) -> p a b", a=8) for i in range(2)]
    Pb = [av(A_SS + 16384 * i + 8192, 4096, BF16, "p (a b) -> p a b", a=8) for i in range(2)]
    PTb = [av(A_SS + 16384 * i + 12288, 4096, BF16, "p (a b) -> p a b", a=16) for i in range(2)]
    ybn = [av(A_SS + 32768 + 2048 * i, 2048, BF16) for i in range(2)]
    actb = [av(16384 * i, 16384, BF16, "p (a b) -> p a b", a=16) for i in range(2)]
    oT = av(0, 32768, F32, "p (a b) -> p a b", a=16)

    gains = cst[:, C_GAIN:C_GAIN + 64]
    sinks = cst[:, C_SINK:C_SINK + 16]
    gnb = cst[:, C_GN:C_GN + 256]
    identf = cst[:, C_ID:C_ID + 128]
    tri4 = cst[:, C_TRI:C_TRI + 512]
    trif = cst[:, C_TRI:C_TRI + 128]
    dist8 = cst[:, C_DIST:C_DIST + 256]

    bank_i = [0]

    def bank():
        b = banks[bank_i[0] % 7]
        bank_i[0] += 1
        return b

    ssq_bank = banks[7]

    sq_pending = []

    def sq_accum(k, delay=3):
        act(sqs[:, k % 4, :], hT[:, k, :], AF.Square)
        sq_pending.append(k)
        while len(sq_pending) > delay:
            sq_flush(1)

    def sq_flush(n=100):
        while sq_pending and n > 0:
            k = sq_pending.pop(0)
            n -= 1
            mm(ssq_bank[:, :], lhsT=onesb[:], rhs=sqs[:, k % 4, :], start=(k == 0), stop=(k == 15), signal=True)

    def mm(out, lhsT, rhs, start=True, stop=True, signal=True):
        return S.op("pe", lambda g: g.matmul(out, lhsT=lhsT, rhs=rhs, start=start, stop=stop),
                    reads=[lhsT, rhs], writes=[out], signal=signal)

    def tr(out, in_, ident, signal=True):
        return S.op("pe", lambda g: g.transpose(out, in_, ident), reads=[in_, ident], writes=[out], signal=signal)

    def act(out, in_, func, bias=None, scale=None, accum=None, eng="act"):
        kw = {}
        rd = [in_]
        wr = [out]
        if bias is not None:
            kw["bias"] = bias
            if not isinstance(bias, (int, float)):
                rd.append(bias)
        if scale is not None:
            kw["scale"] = scale
            if not isinstance(scale, (int, float)):
                rd.append(scale)
        if accum is not None:
            kw["accum_out"] = accum
            wr.append(accum)
        return S.op("act", lambda g: g.activation(out=out, in_=in_, func=func, **kw), reads=rd, writes=wr)

    def copy(eng, out, in_):
        if eng == "act":
            return act(out, in_, AF.Copy)
        return S.op(eng, lambda g: g.tensor_copy(out=out, in_=in_), reads=[in_], writes=[out])

    def tt(out, in0, in1, op, eng="dve"):
        return S.op(eng, lambda g: g.tensor_tensor(out=out, in0=in0, in1=in1, op=op), reads=[in0, in1], writes=[out])

    def ts(out, in0, s1, s2, op0, op1=None, eng="dve"):
        rd = [in0] + [s for s in (s1, s2) if s is not None and not isinstance(s, (int, float))]
        if op1 is None:
            return S.op(eng, lambda g: g.tensor_scalar(out=out, in0=in0, scalar1=s1, scalar2=None, op0=op0),
                        reads=rd, writes=[out])
        return S.op(eng, lambda g: g.tensor_scalar(out=out, in0=in0, scalar1=s1, scalar2=s2, op0=op0, op1=op1),
                    reads=rd, writes=[out])

    def stt(out, in0, scalar, in1, op0, op1):
        rd = [in0, in1] + ([] if isinstance(scalar, (int, float)) else [scalar])
        return S.op("dve", lambda g: g.scalar_tensor_tensor(out=out, in0=in0, scalar=scalar, in1=in1, op0=op0, op1=op1),
                    reads=rd, writes=[out])

    def memset(ap, v, eng="dve"):
        return S.op(eng, lambda g: g.memset(ap, v), writes=[ap])

    csem = [nc.alloc_semaphore("cld"), 0]
    S.dma("sp", cst[:], cst_d[:, :], csem, sb_writes=[cst[:]])
    csem2 = [nc.alloc_semaphore("cld2"), 0]
    S.dma("sp", lnv[0:17, :], wdx_d[:, :], csem2, sb_writes=[lnv[0:17, :]])
    copy("dve", identb[:], identf)
    memset(onesb[:], 1.0)
    copy("dve", wdec[0:17, :], lnv[0:17, :])
    nsink = st[:, 128:144]
    ts(nsink, sinks, -1.0, None, ALU.mult)
    memset(gzx, 1.0)

    NCV = 7
    cv = [[nc.alloc_semaphore("cv%d" % i), 0] for i in range(NCV)]
    cvn = [0]

    def conv(dst, src):
        i = cvn[0]
        cvn[0] += 1
        return S.dma("pool", dst, src, cv[i % NCV])

    gzt = conv(wgz_d.rearrange("p (kc c) -> p kc c", kc=16),
               w_d["w_in"].rearrange("(kc p) c -> p kc c", p=128)[:, :, 3072:3088])
    gsem = [nc.alloc_semaphore("gzl"), 0]
    S.dma("sp", wgz[:], wgz_d.rearrange("p (kc c) -> p kc c", kc=16), gsem, sb_writes=[wgz[:]], extra=[gzt])

    slab_ready = []
    for s, (name, kc, ncols, pcs) in enumerate(plan):
        toks = []
        dv = wsc[s][:, 0:kc * ncols].rearrange("p (kc c) -> p kc c", kc=kc)
        for (wn, r0, c0, n, d0) in pcs:
            src = w_d[wn].rearrange("(kc p) c -> p kc c", p=128)[:, r0:r0 + kc, c0:c0 + n]
            toks.append(conv(dv[:, :, d0:d0 + n], src))
        slab_ready.append(toks)

    rsem = [[nc.alloc_semaphore("ring%d" % i), 0] for i in range(RING)]
    ld = {"n": 0}
    total_slabs = NT * NS
    slab_views = {}

    def issue_slab():
        i = ld["n"]
        if i >= total_slabs:
            return
        ld["n"] += 1
        s = i % NS
        slot = i % RING
        extra = slab_ready[s] if i < NS else ()
        n_el = plan[s][1] * plan[s][2]
        S.dma("sp", ring[:, slot, 0:n_el], wsc[s][:, 0:n_el], rsem[slot], sb_writes=[ring[:, slot, 0:n_el]], extra=extra)

    use = {"n": 0}

    def next_slab(expect, issue=True):
        i = use["n"]
        use["n"] += 1
        s = i % NS
        name, kc, ncols, _ = plan[s]
        assert name == expect, (name, expect)
        if issue:
            issue_slab()
        return ring[:, i % RING, 0:kc * ncols].rearrange("p (kc c) -> p kc c", kc=kc)

    for _ in range(RING - 1):
        issue_slab()

    xsem = [[nc.alloc_semaphore("xs%d" % i), 0] for i in range(8)]
    psem = [nc.alloc_semaphore("pld"), 0]
    osem = [[nc.alloc_semaphore("ost%d" % i), 0] for i in range(2)]
    dbg_toks = []

    def dump(name, ap, dt):
        if not debug:
            return
        shp = list(ap.shape)
        d = nc.dram_tensor("dbg_" + name, shp, dt, kind="ExternalOutput").ap()
        sem = [nc.alloc_semaphore("dbg_" + name), 0]
        dbg_toks.append(S.dma("sp", d, ap, sem, sb_reads=[ap]))

    def load_x(t, m, hf):
        r0 = t * T + m * 128
        dst = xs[:, m, hf * 1024:(hf + 1) * 1024]
        S.dma("sp", dst, x_d[r0:r0 + 128, hf * 1024:(hf + 1) * 1024], xsem[m * 2 + hf], sb_writes=[dst])

    def norm(gcol, writer):
        sq_flush()
        act(lnv, ssq_bank[:, :], AF.Ln, bias=EPS, scale=1.0 / D)
        rb = bank()
        act(rb[:, :], lnv, AF.Exp, scale=-0.5)
        for k in range(16):
            writer(k, rb[:, :], gains[:, gcol + k:gcol + k + 1])

    def norm_to_uT(gcol):
        norm(gcol, lambda k, r, g: stt(uT[:, k, :], hT[:, k, :], g, r, ALU.mult, ALU.mult))

    ev = {"n": 0}

    def evac_eng():
        ev["n"] += 1
        return "act" if ev["n"] % 2 == 0 else "dve"

    def dense_b_pair(sl0, sl1, kc, rhs_of_k):
        pbs = [bank() for _ in range(4)]
        for k in range(kc):
            for c4 in range(4):
                sl = sl0 if c4 < 2 else sl1
                mm(pbs[c4][:, :], lhsT=sl[:, k, 128 * (c4 % 2):128 * (c4 % 2) + 128], rhs=rhs_of_k(k),
                   start=(k == 0), stop=(k == kc - 1), signal=(k == kc - 1))
        return pbs

    def dense_b(slab, kc, c0, rhs_of_k):
        pb = bank()
        for k in range(kc):
            mm(pb[:, :], lhsT=slab[:, k, c0:c0 + 128], rhs=rhs_of_k(k), start=(k == 0), stop=(k == kc - 1),
               signal=(k == kc - 1))
        return pb

    def s1_gen(t):
        first = (t % TPS == 0)
        S.stage = "S1"
        if t == 0:
            for m_ in range(4):
                for hf_ in range(2):
                    load_x(0, m_, hf_)
        S.dma("sp", pst, p_d[t * T:(t + 1) * T, :].rearrange("(m p) c -> p m c", p=128), psem, sb_writes=[pst])
        for m in range(4):
            for kg in range(4):
                S.stage = "S1"
                pb = bank()
                for j in range(4):
                    k = 4 * kg + j
                    tr(pb[:, j * 128:(j + 1) * 128], xs[:, m, k * 128:(k + 1) * 128], identf, signal=(j == 3))
                copy(evac_eng(), hT[:, 4 * kg:4 * kg + 4, m * 128:(m + 1) * 128],
                     pb[:, :].rearrange("p (a b) -> p a b", a=4))
                if m == 3:
                    for k in range(4 * kg, 4 * kg + 4):
                        sq_accum(k)
                yield
            S.stage = "S1"
            pb = bank()
            for j in range(2):
                tr(pb[:, j * 128:(j + 1) * 128], pst[:, m, j * 128:(j + 1) * 128], identf, signal=(j == 1))
            copy(evac_eng(), pT[:, :, m * 128:(m + 1) * 128], pb[:, 0:256].rearrange("p (a b) -> p a b", a=2))
        if first:
            memset(Sst[:], 0.0)
            memset(Sbf[:], 0.0)
        else:
            copy("dve", skx[:, :, 0:128], skx[:, :, 512:640])
            copy("dve", svx[:, 0, :], svx[:, 4, :])

    def s10_gen(t):
        for m in range(4):
            for kg in range(4):
                S.stage = "S10"
                pb = bank()
                for j in range(4):
                    k = 4 * kg + j
                    tr(pb[:, j * 128:(j + 1) * 128], oT[:, k, m * 128:(m + 1) * 128], identf, signal=(j == 3))
                copy(evac_eng(), osb[:, kg * 512:(kg + 1) * 512], pb[:, :])
                if kg % 2 == 1:
                    hf = kg // 2
                    r0 = t * T + m * 128
                    S.dma("sp", out_d[r0:r0 + 128, hf * 1024:(hf + 1) * 1024], osb[:, hf * 1024:(hf + 1) * 1024],
                          osem[hf], sb_reads=[osb[:, hf * 1024:(hf + 1) * 1024]])
                yield

    for _ in s1_gen(0):
        pass
    for t in range(NT):
        first = (t % TPS == 0)
        norm_to_uT(0)

        S.stage = "S2"
        sl0 = next_slab("gq0")
        sl1 = next_slab("gq1", issue=False)
        pbs = dense_b_pair(sl0, sl1, 16, lambda k: uT[:, k, :])
        issue_slab()
        for c4 in range(4):
            copy(evac_eng(), qT[:, c4, :], pbs[c4][:, :])
        for j in range(2):
            sl = next_slab("gk%d" % j)
            for c in range(2):
                pb = dense_b(sl, 16, 128 * c, lambda k: uT[:, k, :])
                copy(evac_eng(), kT[:, 2 * j + c, :], pb[:, :])

        def dense_a(sl, m, half, pb):
            for k in range(16):
                mm(pb[:, half * 256:(half + 1) * 256], lhsT=uT[:, k, m * 128:(m + 1) * 128], rhs=sl[:, k, :],
                   start=(k == 0), stop=(k == 15), signal=(k == 15))

        for j in range(4):
            sl = next_slab("gv%d" % j)
            for mp in range(2):
                pb = bank()
                for hf in range(2):
                    dense_a(sl, 2 * mp + hf, hf, pb)
                copy(evac_eng(), vt[:, 2 * mp:2 * mp + 2, 256 * j:256 * (j + 1)],
                     pb[:, :].rearrange("p (a b) -> p a b", a=2))
        for j in range(4):
            sl = next_slab("gr%d" % j)
            for mp in range(2):
                pb = bank()
                for hf in range(2):
                    dense_a(sl, 2 * mp + hf, hf, pb)
                act(fA[:, 0, :], pb[:, :], AF.Tanh, scale=0.5)
                stt(fA[:, 1, :], fA[:, 0, :], 1.0, pb[:, :], ALU.add, ALU.mult)
                gsl = gnb[:, (256 * j) % 256:(256 * j) % 256 + 256]
                for hf in range(2):
                    tt(G2[:, 2 * mp + hf, 256 * j:256 * (j + 1)], fA[:, 1, hf * 256:(hf + 1) * 256], gsl, ALU.mult)
        pb = bank()
        for k in range(16):
            mm(pb[0:16, :], lhsT=wgz[:, k, :], rhs=uT[:, k, :], start=(k == 0), stop=(k == 15), signal=(k == 15))
        copy("dve", gzx[0:16, :], pb[0:16, :])
        S.stage = "S3"
        def dense_swa_gen():
            for i in range(4):
                sl = next_slab("sq%d" % i)
                for c in range(2):
                    pb = dense_b(sl, 16, 128 * c, lambda k: uT[:, k, :])
                    copy(evac_eng(), sqT[:, 2 * i + c, :], pb[:, :])
                    yield
            sl = next_slab("sk")
            for c in range(2):
                pb = dense_b(sl, 16, 128 * c, lambda k: uT[:, k, :])
                copy(evac_eng(), skx[:, c, 128:640], pb[:, :])
                yield
            sl = next_slab("sv")
            for mp in range(2):
                pb = bank()
                for hf in range(2):
                    dense_a(sl, 2 * mp + hf, hf, pb)
                copy(evac_eng(), svx[:, 1 + 2 * mp:3 + 2 * mp, :], pb[:, :].rearrange("p (a b) -> p a b", a=2))
                yield

        def gla_gen():
            for m in range(4):
                tk = slice(m * 128, (m + 1) * 128)
                pz = bank()
                mm(pz[:, :], lhsT=gzx[0:17, tk], rhs=wdec[0:17, :])
                act(gE2, pz[:, :], AF.Exp, scale=-1.0)
                act(gL, gE2, AF.Ln, bias=1.0)
                yield
                pbt = bank()
                for h in range(4):
                    mm(pbt[:, h * 128:(h + 1) * 128], lhsT=gL[:, h * 128:(h + 1) * 128], rhs=trif, signal=(h == 3))
                act(gE1, pbt[:, :], AF.Exp, scale=-1.0 / 16.0)
                act(gE2, pbt[:, :], AF.Exp, scale=1.0 / 16.0)
                yield
                stt(qtl, qT[:, :, tk], 128.0 ** -0.5, gE1.rearrange("p (a b) -> p a b", a=4), ALU.mult, ALU.mult)
                tt(ktl, kT[:, :, tk], gE2.rearrange("p (a b) -> p a b", a=4), ALU.mult)
                yield
                pk = bank()
                pkb = pk[:, 0:256].bitcast(BF16)
                for h in range(4):
                    tr(pkb[:, h * 128:(h + 1) * 128], ktl[:, h, :], identb[:], signal=(h == 3))
                copy("act", ktm, pkb)
                pa = bank()
                for h in range(4):
                    mm(pa[:, h * 128:(h + 1) * 128], lhsT=ktl[:, h, :], rhs=qtl[:, h, :], signal=(h == 3))
                tt(gAT, pa[:, :].rearrange("p (a b) -> p a b", a=4), tri4.rearrange("p (a b) -> p a b", a=4), ALU.mult)
                yield
                po = [bank(), bank()]
                for h in range(4):
                    o = po[h // 2][:, (h % 2) * 256:(h % 2 + 1) * 256]
                    mm(o, lhsT=gAT[:, h, :], rhs=vt[:, m, h * 256:(h + 1) * 256], start=True, stop=False, signal=False)
                    mm(o, lhsT=qtl[:, h, :], rhs=Sbf[:, h, :], start=False, stop=True, signal=(h % 2 == 1))
                pd = [bank(), bank()]
                for h in range(4):
                    mm(pd[h // 2][:, (h % 2) * 256:(h % 2 + 1) * 256], lhsT=ktm[:, h * 128:(h + 1) * 128],
                       rhs=vt[:, m, h * 256:(h + 1) * 256], signal=(h % 2 == 1))
                for h in range(4):
                    o = po[h // 2][:, (h % 2) * 256:(h % 2 + 1) * 256]
                    act(fA[:, 2, 0:256], o, AF.Square, accum=st[:, h:h + 1])
                act(st[:, 4:8], st[:, 0:4], AF.Ln, bias=EPS, scale=1.0 / 256.0)
                act(st[:, 8:12], st[:, 4:8], AF.Exp, scale=-0.5, bias=math.log(0.5))
                for h in range(4):
                    o = po[h // 2][:, (h % 2) * 256:(h % 2 + 1) * 256]
                    stt(yat[:, h * 256:(h + 1) * 256], o, st[:, 8 + h:9 + h], G2[:, m, h * 256:(h + 1) * 256],
                        ALU.mult, ALU.mult)
                yield
                for i2 in range(2):
                    tt(Sst[:, 2 * i2:2 * i2 + 2, :], Sst[:, 2 * i2:2 * i2 + 2, :],
                       pd[i2][:, :].rearrange("p (a b) -> p a b", a=2), ALU.add)
                dl = gE1[:, 127:128]
                dec = bass.AP(dl.tensor, dl.offset, [list(dl.ap[0]), [128, 4], [0, 256]])
                tt(Sst[:], Sst[:], dec, ALU.mult)
                copy("act", Sbf[:], Sst[:])
                yield
                pt_ = bank()
                ptb = pt_[:, :].bitcast(BF16)
                for c in range(8):
                    tr(ptb[:, c * 128:(c + 1) * 128], yat[:, c * 128:(c + 1) * 128], identb[:], signal=(c == 7))
                copy("act", yaT[:, :, tk], ptb.rearrange("p (a b) -> p a b", a=8))
                yield

        gg = gla_gen()
        gla_live = [True]

        def gla_steps(n):
            for _ in range(n):
                if gla_live[0]:
                    try:
                        next(gg)
                    except StopIteration:
                        gla_live[0] = False

        for _ in dense_swa_gen():
            gla_steps(2)

        def swa_P1(u):
            m, kvp = u // 2, u % 2
            tk = slice(m * 128, (m + 1) * 128)
            k0 = 128 if (first and m == 0) else 0
            b = u % 2
            so = 16 + 56 * kvp
            ps_ = [bank() for _ in range(4)]
            for jj in (0, 4, 1, 5, 2, 6, 3, 7):
                hf, j = jj // 4, jj % 4
                prow = slice(hf * 64, (hf + 1) * 64)
                c = kvp * 4 + j
                mm(ps_[jj // 2][:, (jj % 2) * 256 + k0:(jj % 2 + 1) * 256], lhsT=sqT[prow, c, tk],
                   rhs=skx[prow, kvp, m * 128 + k0:m * 128 + 256], signal=True)
            for jj in range(8):
                h = 8 * kvp + jj
                stt(zz[b][:, jj, k0:256], dist8[:, k0:256], -SLOPES[h],
                    ps_[jj // 2][:, (jj % 2) * 256 + k0:(jj % 2 + 1) * 256], ALU.mult, ALU.add)
            S.op("dve", lambda g, o=st[:, so:so + 8], i=zz[b][:, :, k0:256]: g.tensor_reduce(out=o, in_=i, axis=AX.X, op=ALU.max),
                 reads=[zz[b][:, :, k0:256]], writes=[st[:, so:so + 8]])

        def swa_P1b(u):
            m, kvp = u // 2, u % 2
            k0 = 128 if (first and m == 0) else 0
            b = u % 2
            so = 16 + 56 * kvp
            stt(st[:, so + 8:so + 16], st[:, so:so + 8], -0.125, nsink[:, 8 * kvp:8 * kvp + 8], ALU.mult, ALU.min)
            tt(st[:, so + 16:so + 24], sinks[:, 8 * kvp:8 * kvp + 8], st[:, so + 8:so + 16], ALU.add)
            act(st[:, so + 24:so + 32], st[:, so + 16:so + 24], AF.Exp)
            for jj in range(8):
                act(Pb[b][:, jj, k0:256], zz[b][:, jj, k0:256], AF.Exp, bias=st[:, so + 8 + jj:so + 9 + jj], scale=0.125,
                    accum=st[:, so + 32 + jj:so + 33 + jj])

        def swa_P2(u):
            m, kvp = u // 2, u % 2
            k0 = 128 if (first and m == 0) else 0
            b = u % 2
            so = 16 + 56 * kvp
            tt(st[:, so + 40:so + 48], st[:, so + 24:so + 32], st[:, so + 32:so + 40], ALU.add)
            S.op("dve", lambda g, o=st[:, so + 48:so + 56], i=st[:, so + 40:so + 48]: g.reciprocal(out=o, in_=i),
                 reads=[st[:, so + 40:so + 48]], writes=[st[:, so + 48:so + 56]])
            halves = [1] if k0 else [0, 1]
            n_tr = 4 * len(halves)
            for i2 in range(2):
                pp_ = bank()
                ppb = pp_[:, :].bitcast(BF16)
                cnt = 0
                for j in range(4):
                    jj = 4 * i2 + j
                    for hh in halves:
                        cnt += 1
                        tr(ppb[:, (2 * j + hh) * 128:(2 * j + hh + 1) * 128], Pb[b][:, jj, hh * 128:(hh + 1) * 128],
                           identb[:], signal=(cnt == n_tr))
                dst = PTb[b][:, 8 * i2:8 * i2 + 8, :]
                eng = "act"
                if k0:
                    copy(eng, dst.rearrange("p (j h) q -> p j h q", h=2)[:, :, 1, :],
                         ppb.rearrange("p (j h q) -> p j h q", j=4, h=2)[:, :, 1, :])
                else:
                    copy(eng, dst, ppb.rearrange("p (a b) -> p a b", a=8))

        def swa_P3(u):
            m, kvp = u // 2, u % 2
            tk = slice(m * 128, (m + 1) * 128)
            k0 = 128 if (first and m == 0) else 0
            b = u % 2
            so = 16 + 56 * kvp
            halves = [1] if k0 else [0, 1]
            py = bank()
            for jj in range(8):
                kv = 2 * kvp + jj // 4
                o = py[:, jj * 64:(jj + 1) * 64]
                for ii, hh in enumerate(halves):
                    mm(o, lhsT=PTb[b][:, 2 * jj + hh, :], rhs=svx[:, m + hh, kv * 64:(kv + 1) * 64],
                       start=(ii == 0), stop=(ii == len(halves) - 1),
                       signal=(ii == len(halves) - 1 and jj == 7))
            yb_ = ybn[m % 2]
            tt(yb_[:, 8 * kvp * 64:(8 * kvp + 8) * 64].rearrange("p (a b) -> p a b", a=8),
               py[:, :].rearrange("p (a b) -> p a b", a=8),
               st[:, so + 48:so + 56].unsqueeze(2).broadcast_to([128, 8, 64]), ALU.mult)
            if kvp == 1:
                pt_ = bank()
                ptb = pt_[:, :].bitcast(BF16)
                for c in range(8):
                    tr(ptb[:, c * 128:(c + 1) * 128], yb_[:, c * 128:(c + 1) * 128], identb[:], signal=(c == 7))
                copy("dve", ybT[:, :, tk], ptb.rearrange("p (a b) -> p a b", a=8))

        for s_ in range(8 + 3):
            if s_ < 8:
                swa_P1(s_)
            if 0 <= s_ - 2 < 8:
                swa_P2(s_ - 2)
            if 0 <= s_ - 3 < 8:
                swa_P3(s_ - 3)
            if s_ < 8:
                swa_P1b(s_)
            gla_steps(2)
        gla_steps(100)

        if t == 0:
            dump("uT", uT[:], BF16)
            dump("yaT", yaT, BF16)
            dump("ybT", ybT, BF16)
            dump("qT", qT, BF16)
            dump("vt", vt, BF16)
            dump("G2", G2, BF16)
            dump("sqT", sqT, BF16)
            dump("skx", skx, BF16)
            dump("svx", svx, BF16)
        S.stage = "S4"
        for j in range(8):
            slA = next_slab("A%d" % j)
            pA = [dense_b(slA, 8, 128 * c, lambda k: yaT[:, k, :]) for c in range(2)]
            sga = next_slab("ga%d" % j)
            for c in range(2):
                pg = dense_b(sga, 16, 128 * c, lambda k: uT[:, k, :])
                act(fA[:, 0, :], pg[:, :], AF.Tanh, scale=0.5)
                stt(fA[:, 1 + c, :], fA[:, 0, :], 1.0, pA[c][:, :], ALU.add, ALU.mult)
            slB = next_slab("B%d" % j)
            pB = [dense_b(slB, 8, 128 * c, lambda k: ybT[:, k, :]) for c in range(2)]
            sgb = next_slab("gb%d" % j)
            for c in range(2):
                f = 2 * j + c
                pg = dense_b(sgb, 16, 128 * c, lambda k: uT[:, k, :])
                act(fA[:, 0, :], pg[:, :], AF.Tanh, scale=0.5)
                stt(fA[:, 0, :], fA[:, 0, :], 1.0, pB[c][:, :], ALU.add, ALU.mult)
                tt(mgT[:, f, :], fA[:, 0, :], fA[:, 1 + c, :], ALU.add)

        if t == 0:
            dump("mgT", mgT, BF16)
        S.stage = "S5"
        for j in range(8):
            sl = next_slab("wo%d" % j)
            for c in range(2):
                f = 2 * j + c
                pb = dense_b(sl, 16, 128 * c, lambda k: mgT[:, k, :])
                stt(hT[:, f, :], pb[:, :], 0.5, hT[:, f, :], ALU.mult, ALU.add)
                sq_accum(f)

        if t == 0:
            dump("h1", hT[:], F32)
        S.stage = "S6-8"
        norm_to_uT(16)
        if t + 1 < NT:
            for m_ in range(4):
                for hf_ in range(2):
                    load_x(t + 1, m_, hf_)
        for q in range(4):
            ab = actb[q % 2]
            j_start = 0
            if q == 0:
                sl0 = next_slab("up0_0")
                sl1 = next_slab("up0_1", issue=False)
                pbs = dense_b_pair(sl0, sl1, 16, lambda k: uT[:, k, :])
                issue_slab()
                for c4 in range(4):
                    act(fA[:, c4 % 3, :], pbs[c4][:, :], AF.Square)
                    stt(ab[:, c4, :], pbs[c4][:, :], 0.0, fA[:, c4 % 3, :], ALU.is_gt, ALU.mult)
                j_start = 2
            for j in range(j_start, 8):
                sl = next_slab("up%d_%d" % (q, j))
                for c in range(2):
                    pb = dense_b(sl, 16, 128 * c, lambda k: uT[:, k, :])
                    act(fA[:, (2 * j + c) % 3, :], pb[:, :], AF.Square)
                    stt(ab[:, 2 * j + c, :], pb[:, :], 0.0, fA[:, (2 * j + c) % 3, :], ALU.is_gt, ALU.mult)
            for j in range(8):
                sl = next_slab("dn%d_%d" % (q, j))
                for c in range(2):
                    f = 2 * j + c
                    pb = dense_b(sl, 16, 128 * c, lambda k: ab[:, k, :])
                    tt(hT[:, f, :], pb[:, :], hT[:, f, :], ALU.add)
                    if q == 3:
                        sq_accum(f)

        if t == 0:
            dump("h2", hT[:], F32)
        S.stage = "S9"
        norm_to_uT(32)
        for j in range(8):
            slp = next_slab("pp%d" % j)
            ppb_ = [dense_b(slp, 2, 128 * c, lambda k: pT[:, k, :]) for c in range(2)]
            sl = next_slab("pg%d" % j)
            pgs = None
            if j == 0:
                pgs = [bank(), bank()]
                for k in range(16):
                    for c in range(2):
                        mm(pgs[c][:, :], lhsT=sl[:, k, 128 * c:128 * c + 128], rhs=uT[:, k, :],
                           start=(k == 0), stop=(k == 15), signal=(k == 15))
            for c in range(2):
                f = 2 * j + c
                pp_ = ppb_[c]
                pg = pgs[c] if pgs is not None else dense_b(sl, 16, 128 * c, lambda k: uT[:, k, :])
                act(fA[:, c, :], pg[:, :], AF.Tanh, scale=0.5)
                stt(fA[:, c, :], fA[:, c, :], 1.0, pp_[:, :], ALU.add, ALU.mult)
                stt(hT[:, f, :], fA[:, c, :], 0.5, hT[:, f, :], ALU.mult, ALU.add)
                sq_accum(f)

        if t == 0:
            dump("h3", hT[:], F32)
        S.stage = "S10"
        norm(48, lambda k, r, g: stt(oT[:, k, :], hT[:, k, :], g, r, ALU.mult, ALU.mult))
        g10 = s10_gen(t)
        g1 = s1_gen(t + 1) if t + 1 < NT else iter(())
        live = [True, True]
        while live[0] or live[1]:
            for gi, gg_ in enumerate((g10, g1)):
                if live[gi]:
                    try:
                        next(gg_)
                    except StopIteration:
                        live[gi] = False

    S.wait_only("sp", [(o_[0], o_[1]) for o_ in osem] + dbg_toks)
    for en in ("pe", "act", "dve"):
        assert not getattr(S.eng[en], "pending", False), en

    nc_sched_holder.clear()
    nc_sched_holder.append(S)
    with nc.Block() as block:
        @block.sync
        def _(g):
            S.emit("sp", g)

        @block.gpsimd
        def _(g):
            S.emit("pool", g)

        @block.tensor
        def _(g):
            S.emit("pe", g)

        @block.scalar
        def _(g):
            S.emit("act", g)

        @block.vector
        def _(g):
            S.emit("dve", g)
    return nc


def make_consts(norm_mix, norm_mlp, norm_ple, norm_final, attn_sinks, gla_norm):
    c = np.zeros((128, C_END), np.float32)
    for n, g in enumerate((norm_mix, norm_mlp, norm_ple, norm_final)):
        c[:, C_GAIN + 16 * n:C_GAIN + 16 * (n + 1)] = np.asarray(g, np.float32).reshape(16, 128).T
    c[:, C_SINK:C_SINK + 16] = np.asarray(attn_sinks, np.float32).reshape(1, 16)
    c[:, C_GN:C_GN + 256] = np.asarray(gla_norm, np.float32).reshape(1, 256)
    c[:, C_ID:C_ID + 128] = np.eye(128, dtype=np.float32)
    tri = (np.arange(128)[:, None] <= np.arange(128)[None, :]).astype(np.float32)
    c[:, C_TRI:C_TRI + 512] = np.tile(tri, (1, 4))
    i = np.arange(128)[:, None]
    j = np.arange(256)[None, :]
    dist = i + 128 - j
    ok = (dist >= 0) & (dist < 128)
    c[:, C_DIST:C_DIST + 256] = np.where(ok, 8.0 * dist, 1e9).astype(np.float32)
    return c


_NC_CACHE = {}


def kernel(x, p, norm_mix, w_in, w_decay, b_decay, gla_norm, attn_sinks, w_branch_a, w_branch_b,
           w_out, norm_mlp, w_up, w_down, norm_ple, w_ple_gate, w_ple_proj, norm_final):
    x = np.asarray(x, np.float32)
    B, SEQ, _ = x.shape
    n_cores = N_CORES
    per = B // n_cores
    key = (per, SEQ)
    if key not in _NC_CACHE:
        _NC_CACHE[key] = build(n_seq=per, seq_len=SEQ)
    nc = _NC_CACHE[key]
    cst = make_consts(np.asarray(norm_mix)[0], np.asarray(norm_mlp)[0], np.asarray(norm_ple)[0],
                      np.asarray(norm_final), np.asarray(attn_sinks)[0], np.asarray(gla_norm)[0])
    wdx = np.concatenate([np.asarray(w_decay, np.float32)[0], np.asarray(b_decay, np.float32)[0][None, :]], axis=0)
    wts = {
        "w_in": np.ascontiguousarray(np.asarray(w_in, np.float32)[0]),
        "w_branch_a": np.ascontiguousarray(np.asarray(w_branch_a, np.float32)[0]),
        "w_branch_b": np.ascontiguousarray(np.asarray(w_branch_b, np.float32)[0]),
        "w_out": np.ascontiguousarray(np.asarray(w_out, np.float32)[0]),
        "w_up": np.ascontiguousarray(np.asarray(w_up, np.float32)[0]),
        "w_down": np.ascontiguousarray(np.asarray(w_down, np.float32)[0]),
        "w_ple_gate": np.ascontiguousarray(np.asarray(w_ple_gate, np.float32)[0]),
        "w_ple_proj": np.ascontiguousarray(np.asarray(w_ple_proj, np.float32)[0]),
    }
    pp = np.asarray(p, np.float32)[0]
    in_maps = []
    for c in range(n_cores):
        m = {"x": np.ascontiguousarray(x[c * per:(c + 1) * per].reshape(per * SEQ, D)),
             "p": np.ascontiguousarray(pp[c * per:(c + 1) * per].reshape(per * SEQ, 256)),
             "cst": cst, "wdx": wdx}
        m.update(wts)
        in_maps.append(m)
    res = run_bass_kernel_spmd(nc, in_maps, core_ids=list(range(n_cores)))
    out = np.concatenate([np.asarray(r["out"]).reshape(per, SEQ, D) for r in res.results], axis=0)
    return out.astype(np.float32)
```
